# Optimizing a Trainium2 kernel written in Bass

```python
import jax, jax.numpy as jnp
from jax import lax
import numpy as np

D_MODEL = 1024
BATCH = 8
SEQ = 4096
DEPTH = 1

MIX_WIDTH = D_MODEL
ATTN_WIDTH = MIX_WIDTH // 2
SGU_WIDTH = MIX_WIDTH - ATTN_WIDTH
HEAD_DIM = 64
N_Q_HEADS = ATTN_WIDTH // HEAD_DIM
N_KV_HEADS = 2
Q_PER_KV = N_Q_HEADS // N_KV_HEADS
KV_WIDTH = N_KV_HEADS * HEAD_DIM
WINDOW = 128
BLOCK = 128
ROPE_THETA = 500000.0
ROT_DIM = HEAD_DIM // 4
SGU_CHUNK = 128
N_SGU_GROUPS = 4
SGU_GROUP_DIM = SGU_WIDTH // N_SGU_GROUPS
D_FF = -(-8 * D_MODEL // (3 * 256)) * 256
IN_WIDTH = ATTN_WIDTH + 2 * KV_WIDTH + 2 * SGU_WIDTH
LN_EPS = 1e-5
ALPHA = (2.0 * DEPTH) ** 0.25
BETA = (8.0 * DEPTH) ** -0.25

kernel_name = "hybrid_swa_sink_gmlp_deepnorm_block"


def layer_norm(x, g, b):
    x32 = x.astype(jnp.float32)
    mu = jnp.mean(x32, axis=-1, keepdims=True)
    var = jnp.mean(jnp.square(x32 - mu), axis=-1, keepdims=True)
    y = (x32 - mu) * lax.rsqrt(var + LN_EPS) * g.astype(jnp.float32) + b.astype(jnp.float32)
    return y.astype(x.dtype)


def rope_tables(positions):
    inv_freq = ROPE_THETA ** (-jnp.arange(0, ROT_DIM, 2, dtype=jnp.float32) / ROT_DIM)
    ang = positions.astype(jnp.float32)[..., None] * inv_freq
    return jnp.cos(ang)[:, :, None, :], jnp.sin(ang)[:, :, None, :]


def apply_partial_rope(t, cos, sin):
    half = ROT_DIM // 2
    t1 = t[..., :half].astype(jnp.float32)
    t2 = t[..., half:ROT_DIM].astype(jnp.float32)
    rot = jnp.concatenate([t1 * cos - t2 * sin, t2 * cos + t1 * sin], axis=-1).astype(t.dtype)
    return jnp.concatenate([rot, t[..., ROT_DIM:]], axis=-1)


def sliding_window_attention(q, k, v, sinks):
    b, s = q.shape[0], q.shape[1]
    nb = s // BLOCK
    qb = q.reshape(b, nb, BLOCK, N_KV_HEADS, Q_PER_KV, HEAD_DIM)

    def band(t):
        cur = t.reshape(b, nb, BLOCK, N_KV_HEADS, HEAD_DIM)
        prev = jnp.pad(cur, ((0, 0), (1, 0), (0, 0), (0, 0), (0, 0)))[:, :-1]
        return jnp.concatenate([prev, cur], axis=2)

    kb, vb = band(k), band(v)
    scores = jnp.einsum('bnqkgd,bnskd->bnkgqs', qb, kb).astype(jnp.float32) * (HEAD_DIM ** -0.5)
    qi = jnp.arange(BLOCK)[:, None]
    kj = jnp.arange(2 * BLOCK)[None, :]
    rel = qi + BLOCK - kj
    kpos = jnp.arange(nb)[:, None, None] * BLOCK + kj - BLOCK
    allowed = (rel >= 0) & (rel < WINDOW) & (kpos >= 0)
    scores = jnp.where(allowed[None, :, None, None, :, :], scores, -1e30)
    sink = jnp.broadcast_to(
        sinks.astype(jnp.float32).reshape(N_KV_HEADS, Q_PER_KV)[None, None, :, :, None, None],
        scores.shape[:-1] + (1,))
    probs = jax.nn.softmax(jnp.concatenate([scores, sink], axis=-1), axis=-1)[..., :-1]
    out = jnp.einsum('bnkgqs,bnskd->bnqkgd', probs.astype(v.dtype), vb)
    return out.reshape(b, s, ATTN_WIDTH)


def spatial_gating(su, sv, ln_g, ln_b, w_s, b_s):
    b, s = su.shape[0], su.shape[1]
    nc = s // SGU_CHUNK
    u = jax.nn.gelu(su, approximate=False)
    vv = layer_norm(jax.nn.gelu(sv, approximate=False), ln_g, ln_b)
    vv = vv.reshape(b, nc, SGU_CHUNK, N_SGU_GROUPS, SGU_GROUP_DIM)
    causal = jnp.tril(jnp.ones((SGU_CHUNK, SGU_CHUNK), dtype=w_s.dtype))
    mixed = jnp.einsum('hts,bcshd->bcthd', w_s * causal, vv) + b_s.T[:, :, None]
    return u * mixed.reshape(b, s, SGU_WIDTH)


def swiglu(h, w_gate, w_up, w_down):
    return (jax.nn.silu(h @ w_gate) * (h @ w_up)) @ w_down


def setup_inputs(seed: int = 0) -> dict:
    key = jax.random.key(seed)
    ks = jax.random.split(key, 24)
    f32 = jnp.float32
    L = DEPTH

    def nrm(k, shape, scale):
        return jax.random.normal(k, shape, f32) * scale

    x = jax.random.normal(ks[0], (BATCH, SEQ, D_MODEL), f32)
    positions = jnp.broadcast_to(jnp.arange(SEQ, dtype=jnp.int32)[None, :], (BATCH, SEQ))
    col_scale = jnp.concatenate([
        jnp.ones((ATTN_WIDTH + KV_WIDTH,), f32),
        jnp.full((KV_WIDTH,), BETA, f32),
        jnp.full((2 * SGU_WIDTH,), BETA, f32)])
    w_in = nrm(ks[1], (L, D_MODEL, IN_WIDTH), D_MODEL ** -0.5) * col_scale
    return {
        "x": x,
        "positions": positions,
        "ln_in_g": 1.0 + nrm(ks[2], (D_MODEL,), 0.02),
        "ln_in_b": nrm(ks[3], (D_MODEL,), 0.02),
        "w_in": w_in,
        "b_in": nrm(ks[4], (L, IN_WIDTH), 0.02),
        "attn_sinks": nrm(ks[5], (L, N_Q_HEADS), 0.5),
        "sgu_ln_g": 1.0 + nrm(ks[6], (L, SGU_WIDTH), 0.02),
        "sgu_ln_b": nrm(ks[7], (L, SGU_WIDTH), 0.02),
        "sgu_w": nrm(ks[8], (L, N_SGU_GROUPS, SGU_CHUNK, SGU_CHUNK), 0.5 * SGU_CHUNK ** -0.5),
        "sgu_b": 1.0 + nrm(ks[9], (L, N_SGU_GROUPS, SGU_CHUNK), 0.1),
        "w_out": nrm(ks[10], (L, MIX_WIDTH, D_MODEL), BETA * MIX_WIDTH ** -0.5),
        "b_out": nrm(ks[11], (L, D_MODEL), 0.02),
        "ln_mix_g": 1.0 + nrm(ks[12], (L, D_MODEL), 0.02),
        "ln_mix_b": nrm(ks[13], (L, D_MODEL), 0.02),
        "w_gate": nrm(ks[14], (L, D_MODEL, D_FF), BETA * D_MODEL ** -0.5),
        "w_up": nrm(ks[15], (L, D_MODEL, D_FF), BETA * D_MODEL ** -0.5),
        "w_down": nrm(ks[16], (L, D_FF, D_MODEL), BETA * D_FF ** -0.5),
        "ln_ffn_g": 1.0 + nrm(ks[17], (L, D_MODEL), 0.02),
        "ln_ffn_b": nrm(ks[18], (L, D_MODEL), 0.02),
    }


def reference(x, positions, ln_in_g, ln_in_b, w_in, b_in, attn_sinks, sgu_ln_g, sgu_ln_b,
              sgu_w, sgu_b, w_out, b_out, ln_mix_g, ln_mix_b, w_gate, w_up, w_down,
              ln_ffn_g, ln_ffn_b):
    b, s, _ = x.shape
    cos, sin = rope_tables(positions)
    h = layer_norm(x, ln_in_g, ln_in_b)
    splits = [ATTN_WIDTH, ATTN_WIDTH + KV_WIDTH, ATTN_WIDTH + 2 * KV_WIDTH,
              ATTN_WIDTH + 2 * KV_WIDTH + SGU_WIDTH]
    for l in range(DEPTH):
        proj = h @ w_in[l] + b_in[l]
        q, k, v, su, sv = jnp.split(proj, splits, axis=-1)
        q = apply_partial_rope(q.reshape(b, s, N_Q_HEADS, HEAD_DIM), cos, sin)
        k = apply_partial_rope(k.reshape(b, s, N_KV_HEADS, HEAD_DIM), cos, sin)
        v = v.reshape(b, s, N_KV_HEADS, HEAD_DIM)
        attn = sliding_window_attention(q, k, v, attn_sinks[l])
        sgu = spatial_gating(su, sv, sgu_ln_g[l], sgu_ln_b[l], sgu_w[l], sgu_b[l])
        mix = jnp.concatenate([attn, sgu], axis=-1) @ w_out[l] + b_out[l]
        h = layer_norm(ALPHA * h + mix, ln_mix_g[l], ln_mix_b[l])
        ffn = swiglu(h, w_gate[l], w_up[l], w_down[l])
        h = layer_norm(ALPHA * h + ffn, ln_ffn_g[l], ln_ffn_b[l])
    return h
```

```python
import math
from contextlib import ExitStack

import numpy as np
import concourse.bass as bass
import concourse.mybir as mybir
from concourse.bass_utils import run_bass_kernel_spmd

F32 = mybir.dt.float32
BF16 = mybir.dt.bfloat16
I32 = mybir.dt.int32
AF = mybir.ActivationFunctionType
ALU = mybir.AluOpType

D = 1024
SEQ = 4096
NBLK = 32
TB = 4
INW = 1792
DFF = 2816
ALPHA = 2.0 ** 0.25
EPS = 1e-5
RING = 4
import os
PIPE_MODE = int(os.environ.get('PIPE_MODE', '1'))
STOP_N = int(os.environ.get('STOP_N', '0'))
SKIP = os.environ.get('SKIP', '')


class _Stop(Exception):
    pass


class Buf:
    __slots__ = ("name", "w", "r")

    def __init__(self, name):
        self.name = name
        self.w = None
        self.r = []


class Op:
    __slots__ = ("issuer", "fn", "deps", "signal", "semkey", "sigval", "pos", "isdma", "waits")

    def __init__(self, issuer, fn, isdma, semkey):
        self.issuer = issuer
        self.fn = fn
        self.deps = []
        self.signal = False
        self.semkey = semkey
        self.sigval = None
        self.pos = None
        self.isdma = isdma
        self.waits = None


class Sched:
    ENG = ("pe", "act", "dve", "pool", "sp")

    def __init__(self, nc):
        self.nc = nc
        self.ops = []
        self.streams = {e: [] for e in self.ENG}
        self.dma_counts = {}

    def _add_deps(self, op, reads, writes):
        deps = op.deps
        for b in reads:
            if b.w is not None:
                deps.append((b.w, "raw"))
        for b in writes:
            if b.w is not None:
                deps.append((b.w, "waw"))
            for r in b.r:
                deps.append((r, "war"))
        for b in reads:
            b.r = [r for r in b.r if not (r.issuer == op.issuer and not r.isdma and not op.isdma)]
            b.r.append(op)
        for b in writes:
            b.w = op
            b.r = []

    def op(self, eng, fn, reads=(), writes=()):
        if eng != "pe":
            extra = [b for b in reads if b.name.startswith("pb") and b not in writes]
            if extra:
                writes = list(writes) + extra
        o = Op(eng, fn, False, eng)
        o.pos = len(self.streams[eng])
        self._add_deps(o, reads, writes)
        self.streams[eng].append(o)
        self.ops.append(o)
        return o

    def dma(self, queue, dkey, fn, reads=(), writes=()):
        o = Op(queue, fn, True, "d:" + dkey)
        n = self.dma_counts.get(dkey, 0) + 1
        self.dma_counts[dkey] = n
        o.sigval = 16 * n
        o.pos = len(self.streams[queue])
        self._add_deps(o, reads, writes)
        self.streams[queue].append(o)
        self.ops.append(o)
        return o

    def barrier(self, dkey):
        o = Op("sp", None, True, "d:" + dkey)
        o.sigval = 16 * self.dma_counts[dkey]
        o.pos = 10 ** 9
        return o

    def finalize(self, block, sems, final_dkeys=(), final_eng="sp"):
        for o in self.ops:
            need = {}
            for (p, kind) in o.deps:
                if p is o:
                    continue
                if (not p.isdma) and (not o.isdma) and p.issuer == o.issuer:
                    if o.issuer == "pe":
                        continue
                    if kind != "raw":
                        continue
                    if o.pos - p.pos > 2:
                        continue
                cur = need.get(p.semkey)
                if cur is None or (p.isdma and p.sigval > cur.sigval) or ((not p.isdma) and p.pos > cur.pos):
                    need[p.semkey] = p
            o.waits = list(need.values())
            for p in o.waits:
                p.signal = True
        for e in self.ENG:
            c = 0
            for o in self.streams[e]:
                if not o.isdma and o.signal:
                    c += 1
                    o.sigval = c
        deco = {"pe": block.tensor, "act": block.scalar, "dve": block.vector,
                "pool": block.gpsimd, "sp": block.sync}

        def emit(e, eng):
            waited = {}
            for o in self.streams[e]:
                ws = []
                for p in o.waits:
                    if waited.get(p.semkey, 0) >= p.sigval:
                        continue
                    waited[p.semkey] = p.sigval
                    ws.append((sems[p.semkey], p.sigval))
                for (s, v) in ws[:-1]:
                    eng.wait_ge(s, v)
                ins = o.fn(eng)
                if ws:
                    ins._wait_ge(ws[-1][0], ws[-1][1])
                if o.isdma:
                    ins.then_inc(sems[o.semkey], 16)
                elif o.signal:
                    ins.then_inc(sems[o.semkey], 1)
            if e == final_eng:
                for k in final_dkeys:
                    eng.wait_ge(sems["d:" + k], 16 * self.dma_counts[k])

        for e in self.ENG:
            if self.streams[e] or e == final_eng:
                deco[e](lambda eng, e=e: emit(e, eng))


def build_nc(NT=8):
    nc = bass.Bass("TRN2", target_bir_lowering=False)

    def di(name, shape, dt=F32):
        return nc.dram_tensor(name, shape, dt, kind="ExternalInput").ap()

    x = di("x", [SEQ, D])
    pos = di("pos", [SEQ], I32)
    ln_in_g = di("ln_in_g", [D]); ln_in_b = di("ln_in_b", [D])
    w_in = di("w_in", [D, INW]); b_in = di("b_in", [INW])
    sinks = di("sinks", [8])
    sgu_ln_g = di("sgu_ln_g", [512]); sgu_ln_b = di("sgu_ln_b", [512])
    sgu_w = di("sgu_w", [4, 128, 128]); sgu_b = di("sgu_b", [4, 128])
    w_out = di("w_out", [D, D]); b_out = di("b_out", [D])
    ln_mix_g = di("ln_mix_g", [D]); ln_mix_b = di("ln_mix_b", [D])
    w_gate = di("w_gate", [D, DFF]); w_up = di("w_up", [D, DFF]); w_down = di("w_down", [DFF, D])
    ln_ffn_g = di("ln_ffn_g", [D]); ln_ffn_b = di("ln_ffn_b", [D])
    cmask = di("cmask", [4, 128, 128])
    invf = di("invf", [8])
    y = nc.dram_tensor("y", [SEQ, D], F32, kind="ExternalOutput").ap()
    scr_g = nc.dram_tensor("scr_g", [D, DFF], BF16).ap()
    scr_u = nc.dram_tensor("scr_u", [D, DFF], BF16).ap()
    scr_d = nc.dram_tensor("scr_d", [DFF, D], BF16).ap()

    with ExitStack() as es:
        def sb(name, shape, dt):
            return es.enter_context(nc.sbuf_tensor(name, shape, dt))

        winb = sb("winb", [128, 8, INW], BF16)
        woutb = sb("woutb", [128, 8, D], BF16)
        ring = [sb(f"ring{i}", [128, 2048], BF16) for i in range(RING)]
        Gin = sb("Gin", [128, D], F32); Bin = sb("Bin", [128, D], F32); Cres = sb("Cres", [128, D], F32)
        Gmix = sb("Gmix", [128, D], F32); Bmix = sb("Bmix", [128, D], F32)
        Gffn = sb("Gffn", [128, D], F32); Bffn = sb("Bffn", [128, D], F32)
        Gs = sb("Gs", [128, 512], F32); Bs = sb("Bs", [128, 512], F32); bsb = sb("bsb", [128, 512], F32)
        identb = sb("identb", [128, 128], BF16)
        maskb = sb("maskb", [128, 2, 128], BF16)
        Wm = sb("Wm", [128, 4, 128], BF16); WcT = sb("WcT", [128, 4, 128], BF16)
        esk = sb("esk", [128, 4], F32); bsu = sb("bsu", [128, 4], F32); epsT = sb("epsT", [128, 1], F32)
        brow = sb("brow", [33, INW], BF16); ones33 = sb("ones33", [33, 128], BF16)
        ones64 = sb("ones64", [128, 64], BF16)
        posi = sb("posi", [128, 32], I32); posf = sb("posf", [128, 32], F32)
        invf8 = sb("invf8", [128, 8], F32)
        CS = sb("CS", [128, 32, 16], F32); SS = sb("SS", [128, 32, 16], F32)
        XB = [sb(f"XB{i}", [128, D], F32) for i in range(2)]
        HR = [sb(f"HR{i}", [128, D], F32) for i in range(4)]
        hbA = [sb(f"hbA{i}", [128, D], BF16) for i in range(1)]
        hbC = sb("hbC", [128, D], BF16)
        hT = [sb(f"hT{i}", [128, 8, 128], BF16) for i in range(1)]
        h2 = sb("h2", [128, TB, D], F32)
        h2T = sb("h2T", [128, 8, 512], BF16)
        actT = sb("actT", [128, 22, 512], BF16)
        qr = sb("qr", [128, 8, 64], BF16); kr = sb("kr", [128, 2, 64], BF16)
        rt = sb("rt", [128, 8, 16], F32); ra = sb("ra", [128, 8, 16], F32)
        vb = sb("vb", [128, 4, 128], BF16)
        qT = [sb(f"qT{i}", [128, 512], BF16) for i in range(3)]; kT = sb("kT", [128, 4, 128], BF16)
        uT = [sb(f"uT{i}", [128, 4, 128], BF16) for i in range(3)]
        gs = sb("gs", [128, 512], F32); vv = [sb(f"vv{i}", [128, 512], BF16) for i in range(3)]
        Et = [sb(f"E{i}", [128, 512], BF16) for i in range(2)]
        dn = sb("dn", [128, 512], F32)
        catT = [sb(f"catT{i}", [128, 8, 128], BF16) for i in range(1)]
        sg = [sb(f"sg{i}", [128, 512], F32) for i in range(1)]
        NST = 8
        stt = [sb(f"stt{i}", [128, 12], F32) for i in range(NST)]
        mvt = [sb(f"mvt{i}", [128, 4], F32) for i in range(NST)]
        ps = es.enter_context(nc.psum_tensor("ps", [128, 4096], F32))
        actflat = actT[:].rearrange("p j t -> p (j t)")
        actf = actflat.bitcast(F32)
        acti = actflat.bitcast(I32)
        A2 = actf[:, 0:512]; KF = actf[:, 512:1024]; SN = actf[:, 1024:1536]; KI = acti[:, 1536:2048]
        W4 = actf[:, 2048:2560].rearrange("p (h s) -> p h s", h=4)
        cmf = actf[:, 2560:3072].rearrange("p (c q) -> p c q", c=4)
        h2f = h2[:].rearrange("p b d -> p (b d)")
        bfull = h2f[0:33, 0:INW]; bhf = h2f[0:33, 2048:2048 + INW]
        tm = dn

        S = Sched(nc)
        dkeys = (["cvG", "cvU", "cvD", "wi", "wo", "c1", "c2", "c3", "x0", "x1", "y0", "y1", "stg0", "stg1"]
                 + [f"ring{i}" for i in range(RING)])
        keys = ["pe", "act", "dve", "pool"] + ["d:" + k for k in dkeys]
        sems = {k: es.enter_context(nc.semaphore(k.replace(":", "_"))) for k in keys}
        block = es.enter_context(nc.Block())

        B = {}

        def bf(n):
            if n not in B:
                B[n] = Buf(n)
            return B[n]

        pb = [bf(f"pb{i}") for i in range(8)]

        def bank(i, n=1):
            return ps[:, i * 512:(i + n) * 512]

        psb3 = bank(3).bitcast(BF16)

        def c1(out, in_, **kw):
            S.dma("sp", "c1", lambda e: e.dma_start(out=out, in_=in_, **kw))

        late = []

        def c3(out, in_, **kw):
            late.append((out, in_, kw))

        S.op("dve", lambda e: e.memset(bfull, 0.0), writes=[bf("bfull")])
        c1(Gin[:], ln_in_g.partition_broadcast(128))
        c1(Bin[:], ln_in_b.partition_broadcast(128))
        c1(cmf, cmask.rearrange("c p q -> p c q"))
        c3(posi[:], pos.rearrange("(b p) -> p b", p=128), allow_slow_non_contiguous=True)
        c3(invf8[:], invf.partition_broadcast(128))
        for r in (0, 32):
            S.dma("sp", "c1", lambda e, r=r: e.dma_start(
                out=bfull[r:r + 1, 512:INW], in_=b_in[512:INW].rearrange("(o n) -> o n", o=1)),
                writes=[bf("bfull")])
            for g in range(2):
                S.dma("sp", "c1", lambda e, r=r, g=g: e.dma_start(
                    out=bfull[r:r + 1, 0:512].rearrange("o (hq g d) -> o hq g d", g=2, d=64)[:, :, g, :],
                    in_=b_in[g * 256:(g + 1) * 256].rearrange("(o hq d) -> o hq d", o=1, d=64)),
                    writes=[bf("bfull")])
        c1(bsu[:], b_in[768:1280].rearrange("(c p) -> p c", p=128), allow_slow_non_contiguous=True)
        c3(esk[0:64, :], sinks[0:4].partition_broadcast(64))
        c3(esk[64:128, :], sinks[4:8].partition_broadcast(64))
        c1(Cres[:], b_out.partition_broadcast(128))
        c3(W4, sgu_w.rearrange("h t s -> t h s"))
        c3(Gs[:], sgu_ln_g.partition_broadcast(128))
        c3(Bs[:], sgu_ln_b.partition_broadcast(128))
        c3(bsb[:], sgu_b.rearrange("h t -> (h t)").partition_broadcast(128))
        c3(Gmix[:], ln_mix_g.partition_broadcast(128))
        c3(Bmix[:], ln_mix_b.partition_broadcast(128))
        c3(Gffn[:], ln_ffn_g.partition_broadcast(128))
        c3(Bffn[:], ln_ffn_b.partition_broadcast(128))
        bar_c1 = S.barrier("c1")
        for n in ["Gin", "Bin", "cmf", "bfull", "bsu", "Cres"]:
            bf(n).w = bar_c1

        stg = [XB[0], XB[1]]
        for kk in range(4):
            st_t = stg[kk % 2]
            bst = bf(f"XB{kk % 2}")
            for g in range(2):
                for k2 in range(2):
                    S.dma("sp", f"stg{kk % 2}", lambda e, kk=kk, g=g, k2=k2, st_t=st_t: e.dma_start(
                        out=st_t[:].rearrange("p (k hq g d) -> p k hq g d", k=2, g=2, d=64)[:, k2, :, g, :],
                        in_=w_in[(2 * kk + k2) * 128:(2 * kk + k2 + 1) * 128, g * 256:(g + 1) * 256].rearrange(
                            "p (hq d) -> p hq d", d=64)), writes=[bst])
            bst.w = S.barrier(f"stg{kk % 2}")
            S.op("act", lambda e, kk=kk, st_t=st_t: e.activation(
                out=winb[:, 2 * kk:2 * kk + 2, 0:512], in_=st_t[:].rearrange("p (k n) -> p k n", k=2), func=AF.Copy),
                reads=[bst], writes=[bf("winq")])
        for (o_, i_, kw_) in late:
            S.dma("sp", "c3", lambda e, o_=o_, i_=i_, kw_=kw_: e.dma_start(out=o_, in_=i_, **kw_))
        bar_c3 = S.barrier("c3")
        for n in ["posi", "invf8", "esk", "W4", "Gs", "Bs", "bsb", "Gmix", "Bmix", "Gffn", "Bffn"]:
            bf(n).w = bar_c3
        for kk in range(4):
            S.dma("pool", "wi", lambda e, kk=kk: e.dma_start(
                out=winb[:, 2 * kk:2 * kk + 2, 512:INW],
                in_=w_in[kk * 256:(kk + 1) * 256, 512:INW].rearrange("(k p) n -> p k n", p=128)))
        bf("winr").w = S.barrier("wi")
        for g in range(2):
            S.dma("pool", "wo", lambda e, g=g: e.dma_start(
                out=woutb[g * 64:(g + 1) * 64, 0:4, :],
                in_=w_out[g * 256:(g + 1) * 256, :].rearrange("(hq d) n -> d hq n", d=64)))
        S.dma("pool", "wo", lambda e: e.dma_start(
            out=woutb[:, 4:8, :], in_=w_out[512:1024, :].rearrange("(c p) n -> p c n", p=128)))
        bf("wout").w = S.barrier("wo")
        def emit_conversions():
            for (src, dst, key, rows) in ((w_gate, scr_g, "cvG", D), (w_up, scr_u, "cvU", D), (w_down, scr_d, "cvD", DFF)):
                npc = 4
                step = rows // npc
                for i in range(npc):
                    S.dma("pool", key, lambda e, src=src, dst=dst, i=i, step=step: e.dma_start(
                        out=dst[i * step:(i + 1) * step, :], in_=src[i * step:(i + 1) * step, :]))
                bf("scr_" + key).w = S.barrier(key)

        S.op("dve", lambda e: e.memset(epsT[:], EPS), writes=[bf("epsT")])
        S.op("dve", lambda e: e.memset(ones33[:], 1.0), writes=[bf("ones33")])
        S.op("dve", lambda e: e.memset(ones64[:], 1.0), writes=[bf("ones64")])
        S.op("dve", lambda e: e.scalar_tensor_tensor(out=Cres[:], in0=Bin[:], scalar=ALPHA, in1=Cres[:],
                                                     op0=ALU.mult, op1=ALU.add),
             reads=[bf("Bin"), bf("Cres")], writes=[bf("Cres")])
        S.op("dve", lambda e: e.tensor_copy(out=identb[:], in_=cmf[:, 0, :]), reads=[bf("cmf")], writes=[bf("identb")])
        S.op("dve", lambda e: e.tensor_copy(out=maskb[:, 0, :], in_=cmf[:, 2, :]), reads=[bf("cmf")], writes=[bf("maskb")])
        S.op("dve", lambda e: e.tensor_copy(out=maskb[:, 1, :], in_=cmf[:, 1, :]), reads=[bf("cmf")], writes=[bf("maskb")])
        S.op("dve", lambda e: e.tensor_copy(out=brow[:], in_=bfull), reads=[bf("bfull")], writes=[bf("brow")])
        S.op("dve", lambda e: e.tensor_copy(out=bhf, in_=brow[:]), reads=[bf("brow")], writes=[bf("bhf")])
        S.op("dve", lambda e: e.tensor_tensor(out=bhf, in0=bfull, in1=bhf, op=ALU.subtract),
             reads=[bf("bfull"), bf("bhf")], writes=[bf("bhf")])
        S.op("dve", lambda e: e.tensor_copy(out=brow[32:33, :], in_=bhf[32:33, :]), reads=[bf("bhf")], writes=[bf("brow")])
        def setup_b():
            S.op("dve", lambda e: e.tensor_tensor(out=Wm[:], in0=W4,
                                                  in1=cmf[:, 3, :].unsqueeze(1).to_broadcast([128, 4, 128]), op=ALU.mult),
                 reads=[bf("W4"), bf("cmf")], writes=[bf("Wm")])
            for h in range(4):
                S.op("pe", lambda e, h=h: e.transpose(psb3[:, h * 128:(h + 1) * 128], Wm[:, h, :], identb[:]),
                     reads=[bf("Wm"), bf("identb")], writes=[pb[3]])
            S.op("act", lambda e: e.activation(out=WcT[:].rearrange("p h t -> p (h t)"), in_=psb3[:, 0:512], func=AF.Copy),
                 reads=[pb[3]], writes=[bf("WcT")])
            S.op("act", lambda e: e.activation(out=esk[:], in_=esk[:], func=AF.Exp), reads=[bf("esk")], writes=[bf("esk")])
            TWO_PI = 2.0 * math.pi
            C1 = 6.28125
            C2 = TWO_PI - C1
            S.op("dve", lambda e: e.tensor_copy(out=posf[:], in_=posi[:]), reads=[bf("posi")], writes=[bf("posf")])
            A2v = A2.rearrange("p (c b i) -> p c b i", c=2, i=8)
            for b_ in range(NBLK):
                S.op("dve", lambda e, b_=b_: e.tensor_scalar(out=A2v[:, 0, b_, :], in0=invf8[:], scalar1=posf[:, b_:b_ + 1],
                                                           scalar2=None, op0=ALU.mult),
                     reads=[bf("invf8"), bf("posf")], writes=[bf("A2")])
            S.op("dve", lambda e: e.tensor_scalar(out=A2[:, 256:512], in0=A2[:, 0:256], scalar1=0.5 * math.pi, scalar2=None,
                                                  op0=ALU.add), reads=[bf("A2")], writes=[bf("A2")])
            S.op("dve", lambda e: e.tensor_scalar(out=KI, in0=A2, scalar1=1.0 / TWO_PI, scalar2=None, op0=ALU.mult),
                 reads=[bf("A2")], writes=[bf("KI")])
            S.op("dve", lambda e: e.tensor_copy(out=KF, in_=KI), reads=[bf("KI")], writes=[bf("KF")])
            S.op("dve", lambda e: e.scalar_tensor_tensor(out=A2, in0=KF, scalar=-C1, in1=A2, op0=ALU.mult, op1=ALU.add),
                 reads=[bf("KF"), bf("A2")], writes=[bf("A2")])
            S.op("dve", lambda e: e.scalar_tensor_tensor(out=A2, in0=KF, scalar=-C2, in1=A2, op0=ALU.mult, op1=ALU.add),
                 reads=[bf("KF"), bf("A2")], writes=[bf("A2")])
            S.op("dve", lambda e: e.tensor_single_scalar(out=KF, in_=A2, scalar=math.pi, op=ALU.is_gt),
                 reads=[bf("A2")], writes=[bf("KF")])
            S.op("dve", lambda e: e.scalar_tensor_tensor(out=A2, in0=KF, scalar=-TWO_PI, in1=A2, op0=ALU.mult, op1=ALU.add),
                 reads=[bf("KF"), bf("A2")], writes=[bf("A2")])
            S.op("dve", lambda e: e.tensor_single_scalar(out=KF, in_=A2, scalar=-math.pi, op=ALU.is_lt),
                 reads=[bf("A2")], writes=[bf("KF")])
            S.op("dve", lambda e: e.scalar_tensor_tensor(out=A2, in0=KF, scalar=TWO_PI, in1=A2, op0=ALU.mult, op1=ALU.add),
                 reads=[bf("KF"), bf("A2")], writes=[bf("A2")])
            S.op("dve", lambda e: e.tensor_scalar(out=A2, in0=A2, scalar1=3.1415925, scalar2=-3.1415925,
                                                  op0=ALU.min, op1=ALU.max), reads=[bf("A2")], writes=[bf("A2")])
            S.op("act", lambda e: e.activation(out=SN, in_=A2, func=AF.Sin), reads=[bf("A2")], writes=[bf("SN")])
            SNv = SN.rearrange("p (c b i) -> p c b i", c=2, i=8)
            S.op("dve", lambda e: e.tensor_copy(out=CS[:, :, 0:8], in_=SNv[:, 1, :, :]), reads=[bf("SN")], writes=[bf("CS")])
            S.op("dve", lambda e: e.tensor_copy(out=CS[:, :, 8:16], in_=SNv[:, 1, :, :]), reads=[bf("SN")], writes=[bf("CS")])
            S.op("dve", lambda e: e.tensor_scalar(out=SS[:, :, 0:8], in0=SNv[:, 0, :, :], scalar1=-1.0, scalar2=None,
                                                  op0=ALU.mult), reads=[bf("SN")], writes=[bf("SS")])
            S.op("dve", lambda e: e.tensor_copy(out=SS[:, :, 8:16], in_=SNv[:, 0, :, :]), reads=[bf("SN")], writes=[bf("SS")])


        units = []
        for n in range(NT):
            for jj in range(11):
                units.append(("g", jj))
                units.append(("u", jj))
            for jj in range(11):
                units.append(("d", jj))
        state = {"loaded": 0}

        def prefetch(upto):
            while state["loaded"] < min(upto, len(units)):
                u = state["loaded"]
                kind, jj = units[u]
                slot = u % RING
                if kind == "d":
                    src = scr_d[jj * 256:(jj + 1) * 256, :].rearrange("(c p) d -> p c d", p=128)
                    dst = ring[slot][:].rearrange("p (c d) -> p c d", c=2)
                    sb_ = bf("scr_cvD")
                else:
                    scr = scr_g if kind == "g" else scr_u
                    src = scr[:, jj * 256:(jj + 1) * 256].rearrange("(k p) f -> p k f", p=128)
                    dst = ring[slot][:].rearrange("p (k f) -> p k f", k=8)
                    sb_ = bf("scr_cvG" if kind == "g" else "scr_cvU")
                S.dma("sp", f"ring{slot}", lambda e, src=src, dst=dst: e.dma_start(out=dst, in_=src),
                      reads=[sb_], writes=[bf(f"ring{slot}")])
                state["loaded"] += 1

        def use(u):
            prefetch(u + RING - 1)
            return u % RING

        stc = {"i": 0}

        def ln_stats(src, bsrc, width):
            i = stc["i"] % NST
            stc["i"] += 1
            st, mv = stt[i], mvt[i]
            bst, bmv = bf(f"stt{i}"), bf(f"mvt{i}")
            nch = width // 512
            for c in range(nch):
                S.op("dve", lambda e, c=c: e.bn_stats(out=st[:, 6 * c:6 * c + 6], in_=src[:, c * 512:(c + 1) * 512]),
                     reads=[bsrc], writes=[bst])
            S.op("dve", lambda e: e.bn_aggr(out=mv[:, 0:2], in_=st[:, 0:6 * nch]), reads=[bst], writes=[bmv])
            S.op("act", lambda e: e.activation(out=mv[:, 2:3], in_=mv[:, 1:2], func=AF.Ln, bias=epsT[:, 0:1], scale=1.0),
                 reads=[bmv, bf("epsT")], writes=[bmv])
            S.op("act", lambda e: e.activation(out=mv[:, 2:3], in_=mv[:, 2:3], func=AF.Exp, scale=-0.5),
                 reads=[bmv], writes=[bmv])
            S.op("dve", lambda e: e.tensor_scalar(out=mv[:, 3:4], in0=mv[:, 0:1], scalar1=mv[:, 2:3], scalar2=-1.0,
                                                  op0=ALU.mult, op1=ALU.mult), reads=[bmv], writes=[bmv])
            return mv, bmv

        def xload(gb):
            if gb >= NT * TB:
                return
            S.dma("act", f"x{gb % 2}", lambda e: e.dma_start(out=XB[gb % 2][:], in_=x[gb * 128:(gb + 1) * 128, :]),
                  writes=[bf(f"XB{gb % 2}")])

        def rope(psv, nh, gb, out_t, bout, pbufs):
            ssa = SS[:, gb, 0:8].unsqueeze(1).to_broadcast([128, nh, 8])
            ssb = SS[:, gb, 8:16].unsqueeze(1).to_broadcast([128, nh, 8])
            csb = CS[:, gb, :].unsqueeze(1).to_broadcast([128, nh, 16])
            S.op("dve", lambda e: e.tensor_tensor(out=rt[:, 0:nh, 0:8], in0=psv[:, :, 8:16], in1=ssa, op=ALU.mult),
                 reads=pbufs + [bf("SS")], writes=[bf("rt")])
            S.op("dve", lambda e: e.tensor_tensor(out=rt[:, 0:nh, 8:16], in0=psv[:, :, 0:8], in1=ssb, op=ALU.mult),
                 reads=pbufs + [bf("SS")], writes=[bf("rt")])
            S.op("dve", lambda e: e.tensor_tensor(out=ra[:, 0:nh, :], in0=psv[:, :, 0:16], in1=csb, op=ALU.mult),
                 reads=pbufs + [bf("CS")], writes=[bf("ra")])
            S.op("dve", lambda e: e.tensor_tensor(out=out_t[:, :, 0:16], in0=ra[:, 0:nh, :], in1=rt[:, 0:nh, :], op=ALU.add),
                 reads=[bf("ra"), bf("rt")], writes=[bout])

        pb1a = pb[1]

        def pbs(i):
            return [pb[i]]

        def S1(gb):
            xb = XB[gb % 2]; bxb = bf(f"XB{gb % 2}")
            hr = HR[gb % 4]; bhr = bf(f"HR{gb % 4}")
            hbt = hbA[0]; bhb = bf("hbA0")
            hTt = hT[0]; bhT = bf("hT0")
            qTt = qT[gb % 3]; bqT = bf(f"qT{gb % 3}")
            uTt = uT[gb % 3]; buT = bf(f"uT{gb % 3}")
            vvt = vv[gb % 3]; bvv = bf(f"vv{gb % 3}")
            s3 = gb % 4
            mv, bmv = ln_stats(xb, bxb, D)
            yield
            S.op("dve", lambda e: e.scalar_tensor_tensor(out=xb[:], in0=xb[:], scalar=mv[:, 0:1], in1=Gin[:],
                                                         op0=ALU.subtract, op1=ALU.mult),
                 reads=[bxb, bmv, bf("Gin")], writes=[bxb])
            yield
            S.op("dve", lambda e: e.scalar_tensor_tensor(out=hbt[:], in0=xb[:], scalar=mv[:, 2:3], in1=Bin[:],
                                                         op0=ALU.mult, op1=ALU.add),
                 reads=[bxb, bmv, bf("Bin")], writes=[bhb])
            S.op("dve", lambda e: e.tensor_scalar(out=mv[:, 3:4], in0=mv[:, 2:3], scalar1=ALPHA, scalar2=None, op0=ALU.mult),
                 reads=[bmv], writes=[bmv])
            S.op("dve", lambda e: e.scalar_tensor_tensor(out=hr[:], in0=xb[:], scalar=mv[:, 3:4], in1=Cres[:],
                                                         op0=ALU.mult, op1=ALU.add),
                 reads=[bxb, bmv, bf("Cres")], writes=[bhr])
            xload(gb + 1)
            yield
            for k in range(8):
                S.op("pe", lambda e, k=k: e.transpose(psb3[:, k * 128:(k + 1) * 128], hbt[:, k * 128:(k + 1) * 128], identb[:]),
                     reads=[bhb, bf("identb")], writes=[pb[3]])
            S.op("act", lambda e: e.activation(out=hTt[:].rearrange("p k t -> p (k t)"), in_=psb3[:, 0:1024], func=AF.Copy),
                 reads=[pb[3]], writes=[bhT])
            yield
            for (bk, bufs, c0, c1_, width) in ((0, [pb[0]], 0, 512, 512), (1, [pb1a], 512, 768, 256), (2, [pb[2]], 1280, 1792, 512)):
                outp = ps[:, bk * 512:bk * 512 + width]
                S.op("pe", lambda e, outp=outp, c0=c0, c1_=c1_: e.matmul(outp, lhsT=ones33[:], rhs=brow[:, c0:c1_],
                                                                       start=True, stop=False),
                     reads=[bf("ones33"), bf("brow")], writes=bufs)
                for k in range(8):
                    S.op("pe", lambda e, outp=outp, c0=c0, c1_=c1_, k=k: e.matmul(
                        outp, lhsT=hTt[:, k, :], rhs=winb[:, k, c0:c1_], start=False, stop=(k == 7)),
                        reads=[bhT, bf("winq"), bf("winr")], writes=bufs)
                yield
            S.op("act", lambda e: e.activation(out=gs[:], in_=bank(2), func=AF.Gelu), reads=[pb[2]], writes=[bf("gs")])
            psq = bank(0).rearrange("p (s d) -> p s d", d=64)
            rope(psq, 8, gb, qr, bf("qra"), [pb[0]])
            S.op("act", lambda e: e.activation(out=qr[:, :, 16:64], in_=psq[:, :, 16:64], func=AF.Copy),
                 reads=[pb[0]], writes=[bf("qrb")])
            yield
            for c in range(4):
                outp = ps[:, (2 - 2 * (c % 2)) * 512:(2 - 2 * (c % 2)) * 512 + 128]
                for k in range(8):
                    S.op("pe", lambda e, outp=outp, c=c, k=k: e.matmul(
                        outp, lhsT=winb[:, k, 768 + c * 128:768 + (c + 1) * 128], rhs=hTt[:, k, :],
                        start=(k == 0), stop=(k == 7)), reads=[bhT, bf("winr")], writes=[pb[2 - 2 * (c % 2)]])
                S.op("act", lambda e, outp=outp, c=c: e.activation(out=uTt[:, c, :], in_=outp, func=AF.Gelu,
                                                                 bias=bsu[:, c:c + 1], scale=1.0),
                     reads=[pb[2 - 2 * (c % 2)], bf("bsu")], writes=[buT])
                if c == 1:
                    yield
            mv2, bmv2 = ln_stats(gs, bf("gs"), 512)
            yield
            psk = ps[:, 512:640].rearrange("p (s d) -> p s d", d=64)
            if "a" not in SKIP:
                rope(psk, 2, gb, kr, bf("kra"), [pb1a])
            if "b" not in SKIP:
                S.op("act", lambda e: e.activation(out=kr[:, :, 16:64], in_=psk[:, :, 16:64], func=AF.Copy),
                     reads=[pb1a], writes=[bf("krb")])
            if "c" not in SKIP:
                if "V" in SKIP:
                    S.op("dve", lambda e: e.tensor_copy(out=vb[:, s3, :], in_=ps[:, 640:768]),
                         reads=[pb1a], writes=[bf(f"vb{s3}")])
                else:
                    S.op("act", lambda e: e.activation(out=vb[:, s3, :], in_=ps[:, 640:768], func=AF.Copy),
                         reads=[pb1a], writes=[bf(f"vb{s3}")])
            if "d" not in SKIP:
                S.op("dve", lambda e: e.scalar_tensor_tensor(out=gs[:], in0=gs[:], scalar=mv2[:, 0:1], in1=Gs[:],
                                                         op0=ALU.subtract, op1=ALU.mult),
                 reads=[bf("gs"), bmv2, bf("Gs")], writes=[bf("gs")])
            yield
            S.op("dve", lambda e: e.scalar_tensor_tensor(out=vvt[:], in0=gs[:], scalar=mv2[:, 2:3], in1=Bs[:],
                                                         op0=ALU.mult, op1=ALU.add),
                 reads=[bf("gs"), bmv2, bf("Bs")], writes=[bvv])
            qr2 = qr[:].rearrange("p s d -> p (s d)")
            kr2 = kr[:].rearrange("p s d -> p (s d)")
            for hq in range(4):
                S.op("pe", lambda e, hq=hq: e.transpose(psb3[:, hq * 128:(hq + 1) * 128], qr2[:, hq * 128:(hq + 1) * 128], identb[:]),
                     reads=[bf("qra"), bf("qrb"), bf("identb")], writes=[pb[3]])
            S.op("pe", lambda e: e.transpose(psb3[:, 512:640], kr2[:, 0:128], identb[:]),
                 reads=[bf("kra"), bf("krb"), bf("identb")], writes=[pb[3]])
            S.op("act", lambda e: e.activation(out=qTt[:], in_=psb3[:, 0:512], func=AF.Copy), reads=[pb[3]], writes=[bqT])
            S.op("act", lambda e: e.activation(out=kT[:, s3, :], in_=psb3[:, 512:640], func=AF.Copy),
                 reads=[pb[3]], writes=[bf(f"kT{s3}")])
            yield

        def S2(gb):
            hr = HR[gb % 4]; bhr = bf(f"HR{gb % 4}")
            qTt = qT[gb % 3]; bqT = bf(f"qT{gb % 3}")
            uTt = uT[gb % 3]; buT = bf(f"uT{gb % 3}")
            vvt = vv[gb % 3]; bvv = bf(f"vv{gb % 3}")
            cat = catT[0]; bca = bf("catA0"); bcb = bf("catB0")
            cur = gb % 4
            prev = (gb - 1) % 4
            js = [1] if gb == 0 else [0, 1]
            combos = [(g, j) for g in range(2) for j in js]

            def do_score(idx, g, j):
                sl = cur if j == 1 else prev
                bk = 4 + idx % 2
                S.op("pe", lambda e: e.matmul(bank(bk), lhsT=kT[g * 64:(g + 1) * 64, sl, :], rhs=qTt[g * 64:(g + 1) * 64, :],
                                              start=True, stop=True),
                     reads=[bf(f"kT{sl}"), bqT], writes=[pb[bk]])
                E = Et[idx % 2]
                S.op("act", lambda e: e.activation(out=E[:], in_=bank(bk), func=AF.Exp, scale=0.125),
                     reads=[pb[bk]], writes=[bf(f"E{idx % 2}")])
                E3 = E[:].rearrange("p (h q) -> p h q", q=128)
                S.op("pool", lambda e: e.tensor_tensor(out=E3, in0=E3, in1=maskb[:, j, :].unsqueeze(1).to_broadcast([128, 4, 128]),
                                                       op=ALU.mult),
                     reads=[bf(f"E{idx % 2}"), bf("maskb")], writes=[bf(f"E{idx % 2}")])

            def do_pv(idx, g, j):
                sl = cur if j == 1 else prev
                E = Et[idx % 2]
                first = (j == js[0])
                last = (j == js[-1])
                S.op("pe", lambda e: e.matmul(ps[g * 64:(g + 1) * 64, 6 * 512:7 * 512], lhsT=vb[:, sl, g * 64:(g + 1) * 64],
                                              rhs=E[:], start=first, stop=last),
                     reads=[bf(f"vb{sl}"), bf(f"E{idx % 2}")], writes=[pb[6]])
                S.op("pe", lambda e: e.matmul(ps[g * 64:(g + 1) * 64, 7 * 512:8 * 512], lhsT=ones64[:],
                                              rhs=E[:], start=first, stop=last),
                     reads=[bf("ones64"), bf(f"E{idx % 2}")], writes=[pb[7]])

            nco = len(combos)
            for idx in range(nco + 1):
                if idx < nco:
                    do_score(idx, *combos[idx])
                if idx >= 1:
                    do_pv(idx - 1, *combos[idx - 1])
                yield
            dn3 = dn[:].rearrange("p (h q) -> p h q", q=128)
            S.op("dve", lambda e: e.tensor_tensor(out=dn3, in0=bank(7).rearrange("p (h q) -> p h q", q=128),
                                                  in1=esk[:].unsqueeze(2).to_broadcast([128, 4, 128]), op=ALU.add),
                 reads=[pb[7], bf("esk")], writes=[bf("dn")])
            S.op("dve", lambda e: e.reciprocal(out=dn[:], in_=dn[:]), reads=[bf("dn")], writes=[bf("dn")])
            S.op("dve", lambda e: e.tensor_tensor(out=cat[:, 0:4, :].rearrange("p c t -> p (c t)"), in0=bank(6), in1=dn[:],
                                                  op=ALU.mult), reads=[pb[6], bf("dn")], writes=[bca])
            yield
            for h in range(4):
                S.op("pe", lambda e, h=h: e.matmul(ps[:, 7 * 512 + h * 128:7 * 512 + (h + 1) * 128],
                                                 lhsT=vvt[:, h * 128:(h + 1) * 128], rhs=WcT[:, h, :], start=True, stop=True),
                     reads=[bvv, bf("WcT")], writes=[pb[7]])
            S.op("dve", lambda e: e.tensor_tensor(out=tm[:], in0=bank(7), in1=bsb[:], op=ALU.add),
                 reads=[pb[7], bf("bsb")], writes=[bf("dn")])
            S.op("dve", lambda e: e.tensor_tensor(out=cat[:, 4:8, :].rearrange("p c t -> p (c t)"), in0=tm[:],
                                                  in1=uTt[:].rearrange("p c t -> p (c t)"), op=ALU.mult),
                 reads=[bf("dn"), buT], writes=[bcb])
            yield
            for half in range(2):
                for c in range(8):
                    S.op("pe", lambda e, half=half, c=c: e.matmul(
                        bank(4 + half), lhsT=cat[:, c, :], rhs=woutb[:, c, half * 512:(half + 1) * 512],
                        start=(c == 0), stop=(c == 7)),
                        reads=[bca if c < 4 else bcb, bf("wout")], writes=[pb[4 + half]])
                yield
            S.op("dve", lambda e: e.tensor_tensor(out=hr[:], in0=bank(4, 2), in1=hr[:], op=ALU.add),
                 reads=[pb[4], pb[5], bhr], writes=[bhr])
            yield

        def S3(gb, b):
            hr = HR[gb % 4]; bhr = bf(f"HR{gb % 4}")
            mv3, bmv3 = ln_stats(hr, bhr, D)
            yield
            S.op("dve", lambda e: e.scalar_tensor_tensor(out=hr[:], in0=hr[:], scalar=mv3[:, 0:1], in1=Gmix[:],
                                                         op0=ALU.subtract, op1=ALU.mult),
                 reads=[bhr, bmv3, bf("Gmix")], writes=[bhr])
            yield
            S.op("dve", lambda e: e.scalar_tensor_tensor(out=hbC[:], in0=hr[:], scalar=mv3[:, 2:3], in1=Bmix[:],
                                                         op0=ALU.mult, op1=ALU.add),
                 reads=[bhr, bmv3, bf("Bmix")], writes=[bf("hbC")])
            yield
            S.op("dve", lambda e: e.scalar_tensor_tensor(out=h2[:, b, :], in0=hr[:], scalar=mv3[:, 2:3], in1=Bmix[:],
                                                         op0=ALU.mult, op1=ALU.add),
                 reads=[bhr, bmv3, bf("Bmix")], writes=[bf(f"h2_{b}")])
            yield
            for k in range(8):
                S.op("pe", lambda e, k=k: e.transpose(psb3[:, k * 128:(k + 1) * 128], hbC[:, k * 128:(k + 1) * 128], identb[:]),
                     reads=[bf("hbC"), bf("identb")], writes=[pb[3]])
            S.op("act", lambda e: e.activation(out=h2T[:, :, b * 128:(b + 1) * 128],
                                               in_=psb3[:, 0:1024].rearrange("p (k t) -> p k t", k=8), func=AF.Copy),
                 reads=[pb[3]], writes=[bf("h2T")])
            yield

        def LNF(n):
            for b in range(TB):
                S.op("dve", lambda e, b=b: e.scalar_tensor_tensor(out=h2[:, b, :], in0=h2[:, b, :], scalar=ALPHA,
                                                                in1=bank(2 * b, 2), op0=ALU.mult, op1=ALU.add),
                     reads=[bf(f"h2_{b}")] + pbs(2 * b) + pbs(2 * b + 1), writes=[bf(f"h2_{b}")])
            yield
            for b in range(TB):
                gb = n * TB + b
                yb = h2[:, b, :]; byb = bf(f"h2_{b}")
                mv4, bmv4 = ln_stats(yb, byb, D)
                yield
                S.op("dve", lambda e, yb=yb, mv4=mv4: e.scalar_tensor_tensor(out=yb, in0=yb, scalar=mv4[:, 0:1], in1=Gffn[:],
                                                                           op0=ALU.subtract, op1=ALU.mult),
                     reads=[byb, bmv4, bf("Gffn")], writes=[byb])
                yield
                S.op("dve", lambda e, yb=yb, mv4=mv4: e.scalar_tensor_tensor(out=yb, in0=yb, scalar=mv4[:, 2:3], in1=Bffn[:],
                                                                           op0=ALU.mult, op1=ALU.add),
                     reads=[byb, bmv4, bf("Bffn")], writes=[byb])
                S.dma("sp", "y0", lambda e, yb=yb, gb=gb: e.dma_start(out=y[gb * 128:(gb + 1) * 128, :], in_=yb),
                      reads=[byb])
                yield

        GLEN = {"S1": 8, "S2": 9, "S3": 5, "LNF": 12}

        def run_parallel(gens):
            st = [[g_, 0, GLEN.get(g_.__name__, 8)] for g_ in gens]
            while st:
                st.sort(key=lambda r: r[1] / r[2])
                r = st[0]
                try:
                    next(r[0])
                    r[1] += 1
                except StopIteration:
                    st.remove(r)

        def ffn(n):
            nonlocal_u = ustate
            for jj in range(11):
                sg_slot = use(nonlocal_u["u"]); nonlocal_u["u"] += 1
                su_slot = use(nonlocal_u["u"]); nonlocal_u["u"] += 1
                for c in range(2):
                    j = 2 * jj + c
                    gbk = 2 * (j % 4)
                    ubk = gbk + 1
                    for (bk, sl) in ((gbk, sg_slot), (ubk, su_slot)):
                        for k in range(8):
                            S.op("pe", lambda e, bk=bk, sl=sl, k=k, c=c: e.matmul(
                                bank(bk), lhsT=ring[sl][:, k * 256 + c * 128:k * 256 + (c + 1) * 128], rhs=h2T[:, k, :],
                                start=(k == 0), stop=(k == 7)),
                                reads=[bf(f"ring{sl}"), bf("h2T")], writes=pbs(bk))
                    sgt = sg[0]
                    S.op("act", lambda e, gbk=gbk, sgt=sgt: e.activation(out=sgt[:], in_=bank(gbk), func=AF.Silu),
                         reads=pbs(gbk), writes=[bf("sg0")])
                    S.op("dve", lambda e, ubk=ubk, sgt=sgt, j=j: e.tensor_tensor(out=actT[:, j, :], in0=bank(ubk), in1=sgt[:],
                                                                              op=ALU.mult),
                         reads=pbs(ubk) + [bf("sg0")], writes=[bf("actT")])
            for jj in range(11):
                d_slot = use(nonlocal_u["u"]); nonlocal_u["u"] += 1
                for c in range(2):
                    j = 2 * jj + c
                    for b in range(TB):
                        for half in range(2):
                            S.op("pe", lambda e, d_slot=d_slot, c=c, j=j, b=b, half=half: e.matmul(
                                bank(2 * b + half), lhsT=actT[:, j, b * 128:(b + 1) * 128],
                                rhs=ring[d_slot][:, c * 1024 + half * 512:c * 1024 + (half + 1) * 512],
                                start=(j == 0), stop=(j == 21)),
                                reads=[bf("actT"), bf(f"ring{d_slot}")], writes=pbs(2 * b + half))

        ustate = {"u": 0}
        xload(0)
        NBT = NT * TB
        g0 = S1(0)
        for _ in range(4):
            next(g0)
        setup_b()
        for _ in g0:
            pass
        emit_conversions()
        for _ in S1(1):
            pass
        s1n = 2
        for n in range(NT):
            base = n * TB
            for u in range(5):
                gens = []
                if u == 0 and n > 0:
                    lnf = LNF(n - 1)
                    next(lnf)
                    gens.append(lnf)
                if 1 <= u <= 4:
                    gens.append(S3(base + u - 1, u - 1))
                if u <= 3:
                    gens.append(S2(base + u))
                if u <= 3 and s1n < NBT and s1n <= base + u + 2:
                    gens.append(S1(s1n))
                    s1n += 1
                run_parallel(gens)
            ffn(n)
        for _ in LNF(NT - 1):
            pass

        S.finalize(block, sems, final_dkeys=(["y0"] if "y0" in S.dma_counts else []), final_eng="sp")
    return nc


def _consts():
    p = np.arange(128)[:, None]
    f = np.arange(128)[None, :]
    cm = np.stack([(f == p), (f >= p), (f < p), (f <= p)]).astype(np.float32)
    inv = (500000.0 ** (-np.arange(0, 16, 2, dtype=np.float32) / 16.0)).astype(np.float32)
    return cm, inv


_NC_CACHE = {}


def make_in_maps(inputs, ncores=8):
    cm, inv = _consts()
    f = lambda a: np.ascontiguousarray(np.asarray(a, dtype=np.float32))
    shared = {
        "ln_in_g": f(inputs["ln_in_g"]), "ln_in_b": f(inputs["ln_in_b"]),
        "w_in": f(inputs["w_in"][0]), "b_in": f(inputs["b_in"][0]),
        "sinks": f(inputs["attn_sinks"][0]),
        "sgu_ln_g": f(inputs["sgu_ln_g"][0]), "sgu_ln_b": f(inputs["sgu_ln_b"][0]),
        "sgu_w": f(inputs["sgu_w"][0]), "sgu_b": f(inputs["sgu_b"][0]),
        "w_out": f(inputs["w_out"][0]), "b_out": f(inputs["b_out"][0]),
        "ln_mix_g": f(inputs["ln_mix_g"][0]), "ln_mix_b": f(inputs["ln_mix_b"][0]),
        "w_gate": f(inputs["w_gate"][0]), "w_up": f(inputs["w_up"][0]), "w_down": f(inputs["w_down"][0]),
        "ln_ffn_g": f(inputs["ln_ffn_g"][0]), "ln_ffn_b": f(inputs["ln_ffn_b"][0]),
        "cmask": cm, "invf": inv,
    }
    xs = np.asarray(inputs["x"], dtype=np.float32)
    ps_ = np.asarray(inputs["positions"]).astype(np.int32)
    maps = []
    for c in range(ncores):
        m = dict(shared)
        m["x"] = np.ascontiguousarray(xs[c])
        m["pos"] = np.ascontiguousarray(ps_[c])
        maps.append(m)
    return maps


def kernel(**inputs):
    if "nc" not in _NC_CACHE:
        _NC_CACHE["nc"] = build_nc(8)
    nc = _NC_CACHE["nc"]
    in_maps = make_in_maps(inputs, 8)
    res = run_bass_kernel_spmd(nc, in_maps, core_ids=list(range(8)))
    return np.stack([np.asarray(r["y"], dtype=np.float32) for r in res.results], axis=0)
```

```python
import math
from contextlib import ExitStack

import numpy as np
import concourse.bass as bass
import concourse.mybir as mybir
from concourse.bass_utils import run_bass_kernel_spmd

F32 = mybir.dt.float32
BF16 = mybir.dt.bfloat16
I32 = mybir.dt.int32
AF = mybir.ActivationFunctionType
ALU = mybir.AluOpType

D = 1024
SEQ = 4096
NBLK = 32
TB = 4
INW = 1792
DFF = 2816
ALPHA = 2.0 ** 0.25
EPS = 1e-5
RING = 4
import os
PIPE_MODE = int(os.environ.get('PIPE_MODE', '1'))
STOP_N = int(os.environ.get('STOP_N', '0'))
SKIP = os.environ.get('SKIP', '')


class _Stop(Exception):
    pass


class Buf:
    __slots__ = ("name", "w", "r")

    def __init__(self, name):
        self.name = name
        self.w = None
        self.r = []


class Op:
    __slots__ = ("issuer", "fn", "deps", "signal", "semkey", "sigval", "pos", "isdma", "waits")

    def __init__(self, issuer, fn, isdma, semkey):
        self.issuer = issuer
        self.fn = fn
        self.deps = []
        self.signal = False
        self.semkey = semkey
        self.sigval = None
        self.pos = None
        self.isdma = isdma
        self.waits = None


class Sched:
    ENG = ("pe", "act", "dve", "pool", "sp")

    def __init__(self, nc):
        self.nc = nc
        self.ops = []
        self.streams = {e: [] for e in self.ENG}
        self.dma_counts = {}

    def _add_deps(self, op, reads, writes):
        deps = op.deps
        for b in reads:
            if b.w is not None:
                deps.append((b.w, "raw"))
        for b in writes:
            if b.w is not None:
                deps.append((b.w, "waw"))
            for r in b.r:
                deps.append((r, "war"))
        for b in reads:
            b.r = [r for r in b.r if not (r.issuer == op.issuer and not r.isdma and not op.isdma)]
            b.r.append(op)
        for b in writes:
            b.w = op
            b.r = []

    def op(self, eng, fn, reads=(), writes=()):
        if eng != "pe":
            extra = [b for b in reads if b.name.startswith("pb") and b not in writes]
            if extra:
                writes = list(writes) + extra
        o = Op(eng, fn, False, eng)
        o.pos = len(self.streams[eng])
        self._add_deps(o, reads, writes)
        self.streams[eng].append(o)
        self.ops.append(o)
        return o

    def dma(self, queue, dkey, fn, reads=(), writes=()):
        o = Op(queue, fn, True, "d:" + dkey)
        n = self.dma_counts.get(dkey, 0) + 1
        self.dma_counts[dkey] = n
        o.sigval = 16 * n
        o.pos = len(self.streams[queue])
        self._add_deps(o, reads, writes)
        self.streams[queue].append(o)
        self.ops.append(o)
        return o

    def barrier(self, dkey):
        o = Op("sp", None, True, "d:" + dkey)
        o.sigval = 16 * self.dma_counts[dkey]
        o.pos = 10 ** 9
        return o

    def finalize(self, block, sems, final_dkeys=(), final_eng="sp"):
        for o in self.ops:
            need = {}
            for (p, kind) in o.deps:
                if p is o:
                    continue
                if (not p.isdma) and (not o.isdma) and p.issuer == o.issuer:
                    if o.issuer == "pe":
                        continue
                    if kind != "raw":
                        continue
                    if o.pos - p.pos > 2:
                        continue
                cur = need.get(p.semkey)
                if cur is None or (p.isdma and p.sigval > cur.sigval) or ((not p.isdma) and p.pos > cur.pos):
                    need[p.semkey] = p
            o.waits = list(need.values())
            for p in o.waits:
                p.signal = True
        for e in self.ENG:
            c = 0
            for o in self.streams[e]:
                if not o.isdma and o.signal:
                    c += 1
                    o.sigval = c
        deco = {"pe": block.tensor, "act": block.scalar, "dve": block.vector,
                "pool": block.gpsimd, "sp": block.sync}

        def emit(e, eng):
            waited = {}
            for o in self.streams[e]:
                ws = []
                for p in o.waits:
                    if waited.get(p.semkey, 0) >= p.sigval:
                        continue
                    waited[p.semkey] = p.sigval
                    ws.append((sems[p.semkey], p.sigval))
                for (s, v) in ws[:-1]:
                    eng.wait_ge(s, v)
                ins = o.fn(eng)
                if ws:
                    ins._wait_ge(ws[-1][0], ws[-1][1])
                if o.isdma:
                    ins.then_inc(sems[o.semkey], 16)
                elif o.signal:
                    ins.then_inc(sems[o.semkey], 1)
            if e == final_eng:
                for k in final_dkeys:
                    eng.wait_ge(sems["d:" + k], 16 * self.dma_counts[k])

        for e in self.ENG:
            if self.streams[e] or e == final_eng:
                deco[e](lambda eng, e=e: emit(e, eng))


def build_nc(NT=8):
    nc = bass.Bass("TRN2", target_bir_lowering=False)

    def di(name, shape, dt=F32):
        return nc.dram_tensor(name, shape, dt, kind="ExternalInput").ap()

    x = di("x", [SEQ, D])
    pos = di("pos", [SEQ], I32)
    ln_in_g = di("ln_in_g", [D]); ln_in_b = di("ln_in_b", [D])
    w_in = di("w_in", [D, INW]); b_in = di("b_in", [INW])
    sinks = di("sinks", [8])
    sgu_ln_g = di("sgu_ln_g", [512]); sgu_ln_b = di("sgu_ln_b", [512])
    sgu_w = di("sgu_w", [4, 128, 128]); sgu_b = di("sgu_b", [4, 128])
    w_out = di("w_out", [D, D]); b_out = di("b_out", [D])
    ln_mix_g = di("ln_mix_g", [D]); ln_mix_b = di("ln_mix_b", [D])
    w_gate = di("w_gate", [D, DFF]); w_up = di("w_up", [D, DFF]); w_down = di("w_down", [DFF, D])
    ln_ffn_g = di("ln_ffn_g", [D]); ln_ffn_b = di("ln_ffn_b", [D])
    cmask = di("cmask", [4, 128, 128])
    invf = di("invf", [8])
    y = nc.dram_tensor("y", [SEQ, D], F32, kind="ExternalOutput").ap()
    scr_g = nc.dram_tensor("scr_g", [D, DFF], BF16).ap()
    scr_u = nc.dram_tensor("scr_u", [D, DFF], BF16).ap()
    scr_d = nc.dram_tensor("scr_d", [DFF, D], BF16).ap()

    with ExitStack() as es:
        def sb(name, shape, dt):
            return es.enter_context(nc.sbuf_tensor(name, shape, dt))

        winb = sb("winb", [128, 8, INW], BF16)
        woutb = sb("woutb", [128, 8, D], BF16)
        ring = [sb(f"ring{i}", [128, 2048], BF16) for i in range(RING)]
        Gin = sb("Gin", [128, D], F32); Bin = sb("Bin", [128, D], F32); Cres = sb("Cres", [128, D], F32)
        Gmix = sb("Gmix", [128, D], F32); Bmix = sb("Bmix", [128, D], F32)
        Gffn = sb("Gffn", [128, D], F32); Bffn = sb("Bffn", [128, D], F32)
        Gs = sb("Gs", [128, 512], F32); Bs = sb("Bs", [128, 512], F32); bsb = sb("bsb", [128, 512], F32)
        identb = sb("identb", [128, 128], BF16)
        maskb = sb("maskb", [128, 2, 128], BF16)
        Wm = sb("Wm", [128, 4, 128], BF16); WcT = sb("WcT", [128, 4, 128], BF16)
        esk = sb("esk", [128, 4], F32); bsu = sb("bsu", [128, 4], F32); epsT = sb("epsT", [128, 1], F32)
        brow = sb("brow", [33, INW], BF16); ones33 = sb("ones33", [33, 128], BF16)
        ones64 = sb("ones64", [128, 64], BF16)
        posi = sb("posi", [128, 32], I32); posf = sb("posf", [128, 32], F32)
        invf8 = sb("invf8", [128, 8], F32)
        CS = sb("CS", [128, 32, 16], F32); SS = sb("SS", [128, 32, 16], F32)
        XB = [sb(f"XB{i}", [128, D], F32) for i in range(2)]
        HR = [sb(f"HR{i}", [128, D], F32) for i in range(4)]
        hbA = [sb(f"hbA{i}", [128, D], BF16) for i in range(1)]
        hbC = sb("hbC", [128, D], BF16)
        hT = [sb(f"hT{i}", [128, 8, 128], BF16) for i in range(1)]
        h2 = sb("h2", [128, TB, D], F32)
        h2T = sb("h2T", [128, 8, 512], BF16)
        actT = sb("actT", [128, 22, 512], BF16)
        qr = sb("qr", [128, 8, 64], BF16); kr = sb("kr", [128, 2, 64], BF16)
        rt = sb("rt", [128, 8, 16], F32); ra = sb("ra", [128, 8, 16], F32)
        vb = sb("vb", [128, 4, 128], BF16)
        qT = [sb(f"qT{i}", [128, 512], BF16) for i in range(3)]; kT = sb("kT", [128, 4, 128], BF16)
        uT = [sb(f"uT{i}", [128, 4, 128], BF16) for i in range(3)]
        gs = sb("gs", [128, 512], F32); vv = [sb(f"vv{i}", [128, 512], BF16) for i in range(3)]
        Et = [sb(f"E{i}", [128, 512], BF16) for i in range(2)]
        dn = sb("dn", [128, 512], F32)
        catT = [sb(f"catT{i}", [128, 8, 128], BF16) for i in range(1)]
        sg = [sb(f"sg{i}", [128, 512], F32) for i in range(1)]
        NST = 8
        stt = [sb(f"stt{i}", [128, 12], F32) for i in range(NST)]
        mvt = [sb(f"mvt{i}", [128, 4], F32) for i in range(NST)]
        ps = es.enter_context(nc.psum_tensor("ps", [128, 4096], F32))
        actflat = actT[:].rearrange("p j t -> p (j t)")
        actf = actflat.bitcast(F32)
        acti = actflat.bitcast(I32)
        A2 = actf[:, 0:512]; KF = actf[:, 512:1024]; SN = actf[:, 1024:1536]; KI = acti[:, 1536:2048]
        W4 = actf[:, 2048:2560].rearrange("p (h s) -> p h s", h=4)
        cmf = actf[:, 2560:3072].rearrange("p (c q) -> p c q", c=4)
        h2f = h2[:].rearrange("p b d -> p (b d)")
        bfull = h2f[0:33, 0:INW]; bhf = h2f[0:33, 2048:2048 + INW]
        tm = dn

        S = Sched(nc)
        dkeys = (["cvG", "cvU", "cvD", "wi", "wo", "c1", "c2", "c3", "x0", "x1", "y0", "y1", "stg0", "stg1"]
                 + [f"ring{i}" for i in range(RING)])
        keys = ["pe", "act", "dve", "pool"] + ["d:" + k for k in dkeys]
        sems = {k: es.enter_context(nc.semaphore(k.replace(":", "_"))) for k in keys}
        block = es.enter_context(nc.Block())

        B = {}

        def bf(n):
            if n not in B:
                B[n] = Buf(n)
            return B[n]

        pb = [bf(f"pb{i}") for i in range(8)]

        def bank(i, n=1):
            return ps[:, i * 512:(i + n) * 512]

        psb3 = bank(3).bitcast(BF16)

        def c1(out, in_, **kw):
            S.dma("sp", "c1", lambda e: e.dma_start(out=out, in_=in_, **kw))

        late = []

        def c3(out, in_, **kw):
            late.append((out, in_, kw))

        S.op("dve", lambda e: e.memset(bfull, 0.0), writes=[bf("bfull")])
        c1(Gin[:], ln_in_g.partition_broadcast(128))
        c1(Bin[:], ln_in_b.partition_broadcast(128))
        c1(cmf, cmask.rearrange("c p q -> p c q"))
        c3(posi[:], pos.rearrange("(b p) -> p b", p=128), allow_slow_non_contiguous=True)
        c3(invf8[:], invf.partition_broadcast(128))
        for r in (0, 32):
            S.dma("sp", "c1", lambda e, r=r: e.dma_start(
                out=bfull[r:r + 1, 512:INW], in_=b_in[512:INW].rearrange("(o n) -> o n", o=1)),
                writes=[bf("bfull")])
            for g in range(2):
                S.dma("sp", "c1", lambda e, r=r, g=g: e.dma_start(
                    out=bfull[r:r + 1, 0:512].rearrange("o (hq g d) -> o hq g d", g=2, d=64)[:, :, g, :],
                    in_=b_in[g * 256:(g + 1) * 256].rearrange("(o hq d) -> o hq d", o=1, d=64)),
                    writes=[bf("bfull")])
        c1(bsu[:], b_in[768:1280].rearrange("(c p) -> p c", p=128), allow_slow_non_contiguous=True)
        c3(esk[0:64, :], sinks[0:4].partition_broadcast(64))
        c3(esk[64:128, :], sinks[4:8].partition_broadcast(64))
        c1(Cres[:], b_out.partition_broadcast(128))
        c3(W4, sgu_w.rearrange("h t s -> t h s"))
        c3(Gs[:], sgu_ln_g.partition_broadcast(128))
        c3(Bs[:], sgu_ln_b.partition_broadcast(128))
        c3(bsb[:], sgu_b.rearrange("h t -> (h t)").partition_broadcast(128))
        c3(Gmix[:], ln_mix_g.partition_broadcast(128))
        c3(Bmix[:], ln_mix_b.partition_broadcast(128))
        c3(Gffn[:], ln_ffn_g.partition_broadcast(128))
        c3(Bffn[:], ln_ffn_b.partition_broadcast(128))
        bar_c1 = S.barrier("c1")
        for n in ["Gin", "Bin", "cmf", "bfull", "bsu", "Cres"]:
            bf(n).w = bar_c1

        stg = [XB[0], XB[1]]
        for kk in range(4):
            st_t = stg[kk % 2]
            bst = bf(f"XB{kk % 2}")
            for g in range(2):
                for k2 in range(2):
                    S.dma("sp", f"stg{kk % 2}", lambda e, kk=kk, g=g, k2=k2, st_t=st_t: e.dma_start(
                        out=st_t[:].rearrange("p (k hq g d) -> p k hq g d", k=2, g=2, d=64)[:, k2, :, g, :],
                        in_=w_in[(2 * kk + k2) * 128:(2 * kk + k2 + 1) * 128, g * 256:(g + 1) * 256].rearrange(
                            "p (hq d) -> p hq d", d=64)), writes=[bst])
            bst.w = S.barrier(f"stg{kk % 2}")
            S.op("act", lambda e, kk=kk, st_t=st_t: e.activation(
                out=winb[:, 2 * kk:2 * kk + 2, 0:512], in_=st_t[:].rearrange("p (k n) -> p k n", k=2), func=AF.Copy),
                reads=[bst], writes=[bf("winq")])
        for (o_, i_, kw_) in late:
            S.dma("sp", "c3", lambda e, o_=o_, i_=i_, kw_=kw_: e.dma_start(out=o_, in_=i_, **kw_))
        bar_c3 = S.barrier("c3")
        for n in ["posi", "invf8", "esk", "W4", "Gs", "Bs", "bsb", "Gmix", "Bmix", "Gffn", "Bffn"]:
            bf(n).w = bar_c3
        for kk in range(4):
            S.dma("pool", "wi", lambda e, kk=kk: e.dma_start(
                out=winb[:, 2 * kk:2 * kk + 2, 512:INW],
                in_=w_in[kk * 256:(kk + 1) * 256, 512:INW].rearrange("(k p) n -> p k n", p=128)))
        bf("winr").w = S.barrier("wi")
        for g in range(2):
            S.dma("pool", "wo", lambda e, g=g: e.dma_start(
                out=woutb[g * 64:(g + 1) * 64, 0:4, :],
                in_=w_out[g * 256:(g + 1) * 256, :].rearrange("(hq d) n -> d hq n", d=64)))
        S.dma("pool", "wo", lambda e: e.dma_start(
            out=woutb[:, 4:8, :], in_=w_out[512:1024, :].rearrange("(c p) n -> p c n", p=128)))
        bf("wout").w = S.barrier("wo")
        def emit_conversions():
            for (src, dst, key, rows) in ((w_gate, scr_g, "cvG", D), (w_up, scr_u, "cvU", D), (w_down, scr_d, "cvD", DFF)):
                npc = 4
                step = rows // npc
                for i in range(npc):
                    S.dma("pool", key, lambda e, src=src, dst=dst, i=i, step=step: e.dma_start(
                        out=dst[i * step:(i + 1) * step, :], in_=src[i * step:(i + 1) * step, :]))
                bf("scr_" + key).w = S.barrier(key)

        S.op("dve", lambda e: e.memset(epsT[:], EPS), writes=[bf("epsT")])
        S.op("dve", lambda e: e.memset(ones33[:], 1.0), writes=[bf("ones33")])
        S.op("dve", lambda e: e.memset(ones64[:], 1.0), writes=[bf("ones64")])
        S.op("dve", lambda e: e.scalar_tensor_tensor(out=Cres[:], in0=Bin[:], scalar=ALPHA, in1=Cres[:],
                                                     op0=ALU.mult, op1=ALU.add),
             reads=[bf("Bin"), bf("Cres")], writes=[bf("Cres")])
        S.op("dve", lambda e: e.tensor_copy(out=identb[:], in_=cmf[:, 0, :]), reads=[bf("cmf")], writes=[bf("identb")])
        S.op("dve", lambda e: e.tensor_copy(out=maskb[:, 0, :], in_=cmf[:, 2, :]), reads=[bf("cmf")], writes=[bf("maskb")])
        S.op("dve", lambda e: e.tensor_copy(out=maskb[:, 1, :], in_=cmf[:, 1, :]), reads=[bf("cmf")], writes=[bf("maskb")])
        S.op("dve", lambda e: e.tensor_copy(out=brow[:], in_=bfull), reads=[bf("bfull")], writes=[bf("brow")])
        S.op("dve", lambda e: e.tensor_copy(out=bhf, in_=brow[:]), reads=[bf("brow")], writes=[bf("bhf")])
        S.op("dve", lambda e: e.tensor_tensor(out=bhf, in0=bfull, in1=bhf, op=ALU.subtract),
             reads=[bf("bfull"), bf("bhf")], writes=[bf("bhf")])
        S.op("dve", lambda e: e.tensor_copy(out=brow[32:33, :], in_=bhf[32:33, :]), reads=[bf("bhf")], writes=[bf("brow")])
        def setup_b():
            S.op("dve", lambda e: e.tensor_tensor(out=Wm[:], in0=W4,
                                                  in1=cmf[:, 3, :].unsqueeze(1).to_broadcast([128, 4, 128]), op=ALU.mult),
                 reads=[bf("W4"), bf("cmf")], writes=[bf("Wm")])
            for h in range(4):
                S.op("pe", lambda e, h=h: e.transpose(psb3[:, h * 128:(h + 1) * 128], Wm[:, h, :], identb[:]),
                     reads=[bf("Wm"), bf("identb")], writes=[pb[3]])
            S.op("act", lambda e: e.activation(out=WcT[:].rearrange("p h t -> p (h t)"), in_=psb3[:, 0:512], func=AF.Copy),
                 reads=[pb[3]], writes=[bf("WcT")])
            S.op("act", lambda e: e.activation(out=esk[:], in_=esk[:], func=AF.Exp), reads=[bf("esk")], writes=[bf("esk")])
            TWO_PI = 2.0 * math.pi
            C1 = 6.28125
            C2 = TWO_PI - C1
            S.op("dve", lambda e: e.tensor_copy(out=posf[:], in_=posi[:]), reads=[bf("posi")], writes=[bf("posf")])
            A2v = A2.rearrange("p (c b i) -> p c b i", c=2, i=8)
            for b_ in range(NBLK):
                S.op("dve", lambda e, b_=b_: e.tensor_scalar(out=A2v[:, 0, b_, :], in0=invf8[:], scalar1=posf[:, b_:b_ + 1],
                                                           scalar2=None, op0=ALU.mult),
                     reads=[bf("invf8"), bf("posf")], writes=[bf("A2")])
            S.op("dve", lambda e: e.tensor_scalar(out=A2[:, 256:512], in0=A2[:, 0:256], scalar1=0.5 * math.pi, scalar2=None,
                                                  op0=ALU.add), reads=[bf("A2")], writes=[bf("A2")])
            S.op("dve", lambda e: e.tensor_scalar(out=KI, in0=A2, scalar1=1.0 / TWO_PI, scalar2=None, op0=ALU.mult),
                 reads=[bf("A2")], writes=[bf("KI")])
            S.op("dve", lambda e: e.tensor_copy(out=KF, in_=KI), reads=[bf("KI")], writes=[bf("KF")])
            S.op("dve", lambda e: e.scalar_tensor_tensor(out=A2, in0=KF, scalar=-C1, in1=A2, op0=ALU.mult, op1=ALU.add),
                 reads=[bf("KF"), bf("A2")], writes=[bf("A2")])
            S.op("dve", lambda e: e.scalar_tensor_tensor(out=A2, in0=KF, scalar=-C2, in1=A2, op0=ALU.mult, op1=ALU.add),
                 reads=[bf("KF"), bf("A2")], writes=[bf("A2")])
            S.op("dve", lambda e: e.tensor_single_scalar(out=KF, in_=A2, scalar=math.pi, op=ALU.is_gt),
                 reads=[bf("A2")], writes=[bf("KF")])
            S.op("dve", lambda e: e.scalar_tensor_tensor(out=A2, in0=KF, scalar=-TWO_PI, in1=A2, op0=ALU.mult, op1=ALU.add),
                 reads=[bf("KF"), bf("A2")], writes=[bf("A2")])
            S.op("dve", lambda e: e.tensor_single_scalar(out=KF, in_=A2, scalar=-math.pi, op=ALU.is_lt),
                 reads=[bf("A2")], writes=[bf("KF")])
            S.op("dve", lambda e: e.scalar_tensor_tensor(out=A2, in0=KF, scalar=TWO_PI, in1=A2, op0=ALU.mult, op1=ALU.add),
                 reads=[bf("KF"), bf("A2")], writes=[bf("A2")])
            S.op("dve", lambda e: e.tensor_scalar(out=A2, in0=A2, scalar1=3.1415925, scalar2=-3.1415925,
                                                  op0=ALU.min, op1=ALU.max), reads=[bf("A2")], writes=[bf("A2")])
            S.op("act", lambda e: e.activation(out=SN, in_=A2, func=AF.Sin), reads=[bf("A2")], writes=[bf("SN")])
            SNv = SN.rearrange("p (c b i) -> p c b i", c=2, i=8)
            S.op("dve", lambda e: e.tensor_copy(out=CS[:, :, 0:8], in_=SNv[:, 1, :, :]), reads=[bf("SN")], writes=[bf("CS")])
            S.op("dve", lambda e: e.tensor_copy(out=CS[:, :, 8:16], in_=SNv[:, 1, :, :]), reads=[bf("SN")], writes=[bf("CS")])
            S.op("dve", lambda e: e.tensor_scalar(out=SS[:, :, 0:8], in0=SNv[:, 0, :, :], scalar1=-1.0, scalar2=None,
                                                  op0=ALU.mult), reads=[bf("SN")], writes=[bf("SS")])
            S.op("dve", lambda e: e.tensor_copy(out=SS[:, :, 8:16], in_=SNv[:, 0, :, :]), reads=[bf("SN")], writes=[bf("SS")])


        units = []
        for n in range(NT):
            for jj in range(11):
                units.append(("g", jj))
                units.append(("u", jj))
            for jj in range(11):
                units.append(("d", jj))
        state = {"loaded": 0}

        def prefetch(upto):
            while state["loaded"] < min(upto, len(units)):
                u = state["loaded"]
                kind, jj = units[u]
                slot = u % RING
                if kind == "d":
                    src = scr_d[jj * 256:(jj + 1) * 256, :].rearrange("(c p) d -> p c d", p=128)
                    dst = ring[slot][:].rearrange("p (c d) -> p c d", c=2)
                    sb_ = bf("scr_cvD")
                else:
                    scr = scr_g if kind == "g" else scr_u
                    src = scr[:, jj * 256:(jj + 1) * 256].rearrange("(k p) f -> p k f", p=128)
                    dst = ring[slot][:].rearrange("p (k f) -> p k f", k=8)
                    sb_ = bf("scr_cvG" if kind == "g" else "scr_cvU")
                S.dma("sp", f"ring{slot}", lambda e, src=src, dst=dst: e.dma_start(out=dst, in_=src),
                      reads=[sb_], writes=[bf(f"ring{slot}")])
                state["loaded"] += 1

        def use(u):
            prefetch(u + RING - 1)
            return u % RING

        stc = {"i": 0}

        def ln_stats(src, bsrc, width):
            i = stc["i"] % NST
            stc["i"] += 1
            st, mv = stt[i], mvt[i]
            bst, bmv = bf(f"stt{i}"), bf(f"mvt{i}")
            nch = width // 512
            for c in range(nch):
                S.op("dve", lambda e, c=c: e.bn_stats(out=st[:, 6 * c:6 * c + 6], in_=src[:, c * 512:(c + 1) * 512]),
                     reads=[bsrc], writes=[bst])
            S.op("dve", lambda e: e.bn_aggr(out=mv[:, 0:2], in_=st[:, 0:6 * nch]), reads=[bst], writes=[bmv])
            S.op("act", lambda e: e.activation(out=mv[:, 2:3], in_=mv[:, 1:2], func=AF.Ln, bias=epsT[:, 0:1], scale=1.0),
                 reads=[bmv, bf("epsT")], writes=[bmv])
            S.op("act", lambda e: e.activation(out=mv[:, 2:3], in_=mv[:, 2:3], func=AF.Exp, scale=-0.5),
                 reads=[bmv], writes=[bmv])
            S.op("dve", lambda e: e.tensor_scalar(out=mv[:, 3:4], in0=mv[:, 0:1], scalar1=mv[:, 2:3], scalar2=-1.0,
                                                  op0=ALU.mult, op1=ALU.mult), reads=[bmv], writes=[bmv])
            return mv, bmv

        def xload(gb):
            if gb >= NT * TB:
                return
            S.dma("act", f"x{gb % 2}", lambda e: e.dma_start(out=XB[gb % 2][:], in_=x[gb * 128:(gb + 1) * 128, :]),
                  writes=[bf(f"XB{gb % 2}")])

        def rope(psv, nh, gb, out_t, bout, pbufs):
            ssa = SS[:, gb, 0:8].unsqueeze(1).to_broadcast([128, nh, 8])
            ssb = SS[:, gb, 8:16].unsqueeze(1).to_broadcast([128, nh, 8])
            csb = CS[:, gb, :].unsqueeze(1).to_broadcast([128, nh, 16])
            S.op("dve", lambda e: e.tensor_tensor(out=rt[:, 0:nh, 0:8], in0=psv[:, :, 8:16], in1=ssa, op=ALU.mult),
                 reads=pbufs + [bf("SS")], writes=[bf("rt")])
            S.op("dve", lambda e: e.tensor_tensor(out=rt[:, 0:nh, 8:16], in0=psv[:, :, 0:8], in1=ssb, op=ALU.mult),
                 reads=pbufs + [bf("SS")], writes=[bf("rt")])
            S.op("dve", lambda e: e.tensor_tensor(out=ra[:, 0:nh, :], in0=psv[:, :, 0:16], in1=csb, op=ALU.mult),
                 reads=pbufs + [bf("CS")], writes=[bf("ra")])
            S.op("dve", lambda e: e.tensor_tensor(out=out_t[:, :, 0:16], in0=ra[:, 0:nh, :], in1=rt[:, 0:nh, :], op=ALU.add),
                 reads=[bf("ra"), bf("rt")], writes=[bout])

        pb1a = pb[1]

        def pbs(i):
            return [pb[i]]

        def S1(gb):
            xb = XB[gb % 2]; bxb = bf(f"XB{gb % 2}")
            hr = HR[gb % 4]; bhr = bf(f"HR{gb % 4}")
            hbt = hbA[0]; bhb = bf("hbA0")
            hTt = hT[0]; bhT = bf("hT0")
            qTt = qT[gb % 3]; bqT = bf(f"qT{gb % 3}")
            uTt = uT[gb % 3]; buT = bf(f"uT{gb % 3}")
            vvt = vv[gb % 3]; bvv = bf(f"vv{gb % 3}")
            s3 = gb % 4
            mv, bmv = ln_stats(xb, bxb, D)
            yield
            S.op("dve", lambda e: e.scalar_tensor_tensor(out=xb[:], in0=xb[:], scalar=mv[:, 0:1], in1=Gin[:],
                                                         op0=ALU.subtract, op1=ALU.mult),
                 reads=[bxb, bmv, bf("Gin")], writes=[bxb])
            yield
            S.op("dve", lambda e: e.scalar_tensor_tensor(out=hbt[:], in0=xb[:], scalar=mv[:, 2:3], in1=Bin[:],
                                                         op0=ALU.mult, op1=ALU.add),
                 reads=[bxb, bmv, bf("Bin")], writes=[bhb])
            S.op("dve", lambda e: e.tensor_scalar(out=mv[:, 3:4], in0=mv[:, 2:3], scalar1=ALPHA, scalar2=None, op0=ALU.mult),
                 reads=[bmv], writes=[bmv])
            S.op("dve", lambda e: e.scalar_tensor_tensor(out=hr[:], in0=xb[:], scalar=mv[:, 3:4], in1=Cres[:],
                                                         op0=ALU.mult, op1=ALU.add),
                 reads=[bxb, bmv, bf("Cres")], writes=[bhr])
            xload(gb + 1)
            yield
            for k in range(8):
                S.op("pe", lambda e, k=k: e.transpose(psb3[:, k * 128:(k + 1) * 128], hbt[:, k * 128:(k + 1) * 128], identb[:]),
                     reads=[bhb, bf("identb")], writes=[pb[3]])
            S.op("act", lambda e: e.activation(out=hTt[:].rearrange("p k t -> p (k t)"), in_=psb3[:, 0:1024], func=AF.Copy),
                 reads=[pb[3]], writes=[bhT])
            yield
            for (bk, bufs, c0, c1_, width) in ((0, [pb[0]], 0, 512, 512), (1, [pb1a], 512, 768, 256), (2, [pb[2]], 1280, 1792, 512)):
                outp = ps[:, bk * 512:bk * 512 + width]
                S.op("pe", lambda e, outp=outp, c0=c0, c1_=c1_: e.matmul(outp, lhsT=ones33[:], rhs=brow[:, c0:c1_],
                                                                       start=True, stop=False),
                     reads=[bf("ones33"), bf("brow")], writes=bufs)
                for k in range(8):
                    S.op("pe", lambda e, outp=outp, c0=c0, c1_=c1_, k=k: e.matmul(
                        outp, lhsT=hTt[:, k, :], rhs=winb[:, k, c0:c1_], start=False, stop=(k == 7)),
                        reads=[bhT, bf("winq"), bf("winr")], writes=bufs)
                yield
            S.op("act", lambda e: e.activation(out=gs[:], in_=bank(2), func=AF.Gelu), reads=[pb[2]], writes=[bf("gs")])
            psq = bank(0).rearrange("p (s d) -> p s d", d=64)
            rope(psq, 8, gb, qr, bf("qra"), [pb[0]])
            S.op("act", lambda e: e.activation(out=qr[:, :, 16:64], in_=psq[:, :, 16:64], func=AF.Copy),
                 reads=[pb[0]], writes=[bf("qrb")])
            yield
            for c in range(4):
                outp = ps[:, (2 - 2 * (c % 2)) * 512:(2 - 2 * (c % 2)) * 512 + 128]
                for k in range(8):
                    S.op("pe", lambda e, outp=outp, c=c, k=k: e.matmul(
                        outp, lhsT=winb[:, k, 768 + c * 128:768 + (c + 1) * 128], rhs=hTt[:, k, :],
                        start=(k == 0), stop=(k == 7)), reads=[bhT, bf("winr")], writes=[pb[2 - 2 * (c % 2)]])
                S.op("act", lambda e, outp=outp, c=c: e.activation(out=uTt[:, c, :], in_=outp, func=AF.Gelu,
                                                                 bias=bsu[:, c:c + 1], scale=1.0),
                     reads=[pb[2 - 2 * (c % 2)], bf("bsu")], writes=[buT])
                if c == 1:
                    yield
            mv2, bmv2 = ln_stats(gs, bf("gs"), 512)
            yield
            psk = ps[:, 512:640].rearrange("p (s d) -> p s d", d=64)
            if "a" not in SKIP:
                rope(psk, 2, gb, kr, bf("kra"), [pb1a])
            if "b" not in SKIP:
                S.op("act", lambda e: e.activation(out=kr[:, :, 16:64], in_=psk[:, :, 16:64], func=AF.Copy),
                     reads=[pb1a], writes=[bf("krb")])
            if "c" not in SKIP:
                if "V" in SKIP:
                    S.op("dve", lambda e: e.tensor_copy(out=vb[:, s3, :], in_=ps[:, 640:768]),
                         reads=[pb1a], writes=[bf(f"vb{s3}")])
                else:
                    S.op("act", lambda e: e.activation(out=vb[:, s3, :], in_=ps[:, 640:768], func=AF.Copy),
                         reads=[pb1a], writes=[bf(f"vb{s3}")])
            if "d" not in SKIP:
                S.op("act", lambda e: e.activation(out=gs[:], in_=gs[:], func=AF.Identity, scale=mv2[:, 2:3], bias=mv2[:, 3:4]),
                     reads=[bf("gs"), bmv2], writes=[bf("gs")])
            yield
            S.op("pool", lambda e: e.tensor_tensor(out=gs[:], in0=gs[:], in1=Gs[:], op=ALU.mult),
                 reads=[bf("gs"), bf("Gs")], writes=[bf("gs")])
            S.op("pool", lambda e: e.tensor_tensor(out=vvt[:], in0=gs[:], in1=Bs[:], op=ALU.add),
                 reads=[bf("gs"), bf("Bs")], writes=[bvv])
            qr2 = qr[:].rearrange("p s d -> p (s d)")
            kr2 = kr[:].rearrange("p s d -> p (s d)")
            for hq in range(4):
                S.op("pe", lambda e, hq=hq: e.transpose(psb3[:, hq * 128:(hq + 1) * 128], qr2[:, hq * 128:(hq + 1) * 128], identb[:]),
                     reads=[bf("qra"), bf("qrb"), bf("identb")], writes=[pb[3]])
            S.op("pe", lambda e: e.transpose(psb3[:, 512:640], kr2[:, 0:128], identb[:]),
                 reads=[bf("kra"), bf("krb"), bf("identb")], writes=[pb[3]])
            S.op("act", lambda e: e.activation(out=qTt[:], in_=psb3[:, 0:512], func=AF.Copy), reads=[pb[3]], writes=[bqT])
            S.op("act", lambda e: e.activation(out=kT[:, s3, :], in_=psb3[:, 512:640], func=AF.Copy),
                 reads=[pb[3]], writes=[bf(f"kT{s3}")])
            yield

        def S2(gb):
            hr = HR[gb % 4]; bhr = bf(f"HR{gb % 4}")
            qTt = qT[gb % 3]; bqT = bf(f"qT{gb % 3}")
            uTt = uT[gb % 3]; buT = bf(f"uT{gb % 3}")
            vvt = vv[gb % 3]; bvv = bf(f"vv{gb % 3}")
            cat = catT[0]; bca = bf("catA0"); bcb = bf("catB0")
            cur = gb % 4
            prev = (gb - 1) % 4
            js = [1] if gb == 0 else [0, 1]
            combos = [(g, j) for g in range(2) for j in js]

            def do_score(idx, g, j):
                sl = cur if j == 1 else prev
                bk = 4 + idx % 2
                S.op("pe", lambda e: e.matmul(bank(bk), lhsT=kT[g * 64:(g + 1) * 64, sl, :], rhs=qTt[g * 64:(g + 1) * 64, :],
                                              start=True, stop=True),
                     reads=[bf(f"kT{sl}"), bqT], writes=[pb[bk]])
                E = Et[idx % 2]
                S.op("act", lambda e: e.activation(out=E[:], in_=bank(bk), func=AF.Exp, scale=0.125),
                     reads=[pb[bk]], writes=[bf(f"E{idx % 2}")])
                E3 = E[:].rearrange("p (h q) -> p h q", q=128)
                S.op("pool", lambda e: e.tensor_tensor(out=E3, in0=E3, in1=maskb[:, j, :].unsqueeze(1).to_broadcast([128, 4, 128]),
                                                       op=ALU.mult),
                     reads=[bf(f"E{idx % 2}"), bf("maskb")], writes=[bf(f"E{idx % 2}")])

            def do_pv(idx, g, j):
                sl = cur if j == 1 else prev
                E = Et[idx % 2]
                first = (j == js[0])
                last = (j == js[-1])
                S.op("pe", lambda e: e.matmul(ps[g * 64:(g + 1) * 64, 6 * 512:7 * 512], lhsT=vb[:, sl, g * 64:(g + 1) * 64],
                                              rhs=E[:], start=first, stop=last),
                     reads=[bf(f"vb{sl}"), bf(f"E{idx % 2}")], writes=[pb[6]])
                S.op("pe", lambda e: e.matmul(ps[g * 64:(g + 1) * 64, 7 * 512:8 * 512], lhsT=ones64[:],
                                              rhs=E[:], start=first, stop=last),
                     reads=[bf("ones64"), bf(f"E{idx % 2}")], writes=[pb[7]])

            nco = len(combos)
            for idx in range(nco + 1):
                if idx < nco:
                    do_score(idx, *combos[idx])
                if idx >= 1:
                    do_pv(idx - 1, *combos[idx - 1])
                yield
            dn3 = dn[:].rearrange("p (h q) -> p h q", q=128)
            S.op("dve", lambda e: e.tensor_tensor(out=dn3, in0=bank(7).rearrange("p (h q) -> p h q", q=128),
                                                  in1=esk[:].unsqueeze(2).to_broadcast([128, 4, 128]), op=ALU.add),
                 reads=[pb[7], bf("esk")], writes=[bf("dn")])
            S.op("act", lambda e: e.activation(out=dn[:], in_=dn[:], func=AF.Ln), reads=[bf("dn")], writes=[bf("dn")])
            S.op("act", lambda e: e.activation(out=dn[:], in_=dn[:], func=AF.Exp, scale=-1.0), reads=[bf("dn")], writes=[bf("dn")])
            S.op("dve", lambda e: e.tensor_tensor(out=cat[:, 0:4, :].rearrange("p c t -> p (c t)"), in0=bank(6), in1=dn[:],
                                                  op=ALU.mult), reads=[pb[6], bf("dn")], writes=[bca])
            yield
            for h in range(4):
                S.op("pe", lambda e, h=h: e.matmul(ps[:, 7 * 512 + h * 128:7 * 512 + (h + 1) * 128],
                                                 lhsT=vvt[:, h * 128:(h + 1) * 128], rhs=WcT[:, h, :], start=True, stop=True),
                     reads=[bvv, bf("WcT")], writes=[pb[7]])
            S.op("dve", lambda e: e.tensor_tensor(out=tm[:], in0=bank(7), in1=bsb[:], op=ALU.add),
                 reads=[pb[7], bf("bsb")], writes=[bf("dn")])
            S.op("dve", lambda e: e.tensor_tensor(out=cat[:, 4:8, :].rearrange("p c t -> p (c t)"), in0=tm[:],
                                                  in1=uTt[:].rearrange("p c t -> p (c t)"), op=ALU.mult),
                 reads=[bf("dn"), buT], writes=[bcb])
            yield
            for half in range(2):
                for c in range(8):
                    S.op("pe", lambda e, half=half, c=c: e.matmul(
                        bank(4 + half), lhsT=cat[:, c, :], rhs=woutb[:, c, half * 512:(half + 1) * 512],
                        start=(c == 0), stop=(c == 7)),
                        reads=[bca if c < 4 else bcb, bf("wout")], writes=[pb[4 + half]])
                yield
            S.op("dve", lambda e: e.tensor_tensor(out=hr[:], in0=bank(4, 2), in1=hr[:], op=ALU.add),
                 reads=[pb[4], pb[5], bhr], writes=[bhr])
            yield

        def S3(gb, b):
            hr = HR[gb % 4]; bhr = bf(f"HR{gb % 4}")
            mv3, bmv3 = ln_stats(hr, bhr, D)
            yield
            S.op("act", lambda e: e.activation(out=hr[:], in_=hr[:], func=AF.Identity, scale=mv3[:, 2:3], bias=mv3[:, 3:4]),
                 reads=[bhr, bmv3], writes=[bhr])
            yield
            S.op("pool", lambda e: e.tensor_tensor(out=hr[:], in0=hr[:], in1=Gmix[:], op=ALU.mult),
                 reads=[bhr, bf("Gmix")], writes=[bhr])
            S.op("pool", lambda e: e.tensor_tensor(out=h2[:, b, :], in0=hr[:], in1=Bmix[:], op=ALU.add),
                 reads=[bhr, bf("Bmix")], writes=[bf(f"h2_{b}")])
            yield
            S.op("act", lambda e: e.activation(out=hbC[:], in_=h2[:, b, :], func=AF.Copy),
                 reads=[bf(f"h2_{b}")], writes=[bf("hbC")])
            yield
            for k in range(8):
                S.op("pe", lambda e, k=k: e.transpose(psb3[:, k * 128:(k + 1) * 128], hbC[:, k * 128:(k + 1) * 128], identb[:]),
                     reads=[bf("hbC"), bf("identb")], writes=[pb[3]])
            S.op("act", lambda e: e.activation(out=h2T[:, :, b * 128:(b + 1) * 128],
                                               in_=psb3[:, 0:1024].rearrange("p (k t) -> p k t", k=8), func=AF.Copy),
                 reads=[pb[3]], writes=[bf("h2T")])
            yield

        def LNF(n):
            for b in range(TB):
                S.op("dve", lambda e, b=b: e.scalar_tensor_tensor(out=h2[:, b, :], in0=h2[:, b, :], scalar=ALPHA,
                                                                in1=bank(2 * b, 2), op0=ALU.mult, op1=ALU.add),
                     reads=[bf(f"h2_{b}")] + pbs(2 * b) + pbs(2 * b + 1), writes=[bf(f"h2_{b}")])
            yield
            for b in range(TB):
                gb = n * TB + b
                yb = h2[:, b, :]; byb = bf(f"h2_{b}")
                mv4, bmv4 = ln_stats(yb, byb, D)
                yield
                S.op("act", lambda e, yb=yb, mv4=mv4: e.activation(out=yb, in_=yb, func=AF.Identity,
                                                                 scale=mv4[:, 2:3], bias=mv4[:, 3:4]),
                     reads=[byb, bmv4], writes=[byb])
                yield
                S.op("pool", lambda e, yb=yb: e.tensor_tensor(out=yb, in0=yb, in1=Gffn[:], op=ALU.mult),
                     reads=[byb, bf("Gffn")], writes=[byb])
                S.op("pool", lambda e, yb=yb: e.tensor_tensor(out=yb, in0=yb, in1=Bffn[:], op=ALU.add),
                     reads=[byb, bf("Bffn")], writes=[byb])
                S.dma("sp", "y0", lambda e, yb=yb, gb=gb: e.dma_start(out=y[gb * 128:(gb + 1) * 128, :], in_=yb),
                      reads=[byb])
                yield

        GLEN = {"S1": 8, "S2": 9, "S3": 5, "LNF": 12}

        def run_parallel(gens):
            st = [[g_, 0, GLEN.get(g_.__name__, 8)] for g_ in gens]
            while st:
                st.sort(key=lambda r: r[1] / r[2])
                r = st[0]
                try:
                    next(r[0])
                    r[1] += 1
                except StopIteration:
                    st.remove(r)

        def ffn(n):
            nonlocal_u = ustate
            for jj in range(11):
                sg_slot = use(nonlocal_u["u"]); nonlocal_u["u"] += 1
                su_slot = use(nonlocal_u["u"]); nonlocal_u["u"] += 1
                for c in range(2):
                    j = 2 * jj + c
                    gbk = 2 * (j % 4)
                    ubk = gbk + 1
                    for (bk, sl) in ((gbk, sg_slot), (ubk, su_slot)):
                        for k in range(8):
                            S.op("pe", lambda e, bk=bk, sl=sl, k=k, c=c: e.matmul(
                                bank(bk), lhsT=ring[sl][:, k * 256 + c * 128:k * 256 + (c + 1) * 128], rhs=h2T[:, k, :],
                                start=(k == 0), stop=(k == 7)),
                                reads=[bf(f"ring{sl}"), bf("h2T")], writes=pbs(bk))
                    sgt = sg[0]
                    S.op("act", lambda e, gbk=gbk, sgt=sgt: e.activation(out=sgt[:], in_=bank(gbk), func=AF.Silu),
                         reads=pbs(gbk), writes=[bf("sg0")])
                    S.op("dve", lambda e, ubk=ubk, sgt=sgt, j=j: e.tensor_tensor(out=actT[:, j, :], in0=bank(ubk), in1=sgt[:],
                                                                              op=ALU.mult),
                         reads=pbs(ubk) + [bf("sg0")], writes=[bf("actT")])
            for jj in range(11):
                d_slot = use(nonlocal_u["u"]); nonlocal_u["u"] += 1
                for c in range(2):
                    j = 2 * jj + c
                    for b in range(TB):
                        for half in range(2):
                            S.op("pe", lambda e, d_slot=d_slot, c=c, j=j, b=b, half=half: e.matmul(
                                bank(2 * b + half), lhsT=actT[:, j, b * 128:(b + 1) * 128],
                                rhs=ring[d_slot][:, c * 1024 + half * 512:c * 1024 + (half + 1) * 512],
                                start=(j == 0), stop=(j == 21)),
                                reads=[bf("actT"), bf(f"ring{d_slot}")], writes=pbs(2 * b + half))

        ustate = {"u": 0}
        xload(0)
        NBT = NT * TB
        g0 = S1(0)
        for _ in range(4):
            next(g0)
        setup_b()
        for _ in g0:
            pass
        emit_conversions()
        for _ in S1(1):
            pass
        s1n = 2
        for n in range(NT):
            base = n * TB
            for u in range(5):
                gens = []
                if u == 0 and n > 0:
                    lnf = LNF(n - 1)
                    next(lnf)
                    gens.append(lnf)
                if 1 <= u <= 4:
                    gens.append(S3(base + u - 1, u - 1))
                if u <= 3:
                    gens.append(S2(base + u))
                if u <= 3 and s1n < NBT and s1n <= base + u + 2:
                    gens.append(S1(s1n))
                    s1n += 1
                run_parallel(gens)
            ffn(n)
        for _ in LNF(NT - 1):
            pass

        S.finalize(block, sems, final_dkeys=(["y0"] if "y0" in S.dma_counts else []), final_eng="sp")
    return nc


def _consts():
    p = np.arange(128)[:, None]
    f = np.arange(128)[None, :]
    cm = np.stack([(f == p), (f >= p), (f < p), (f <= p)]).astype(np.float32)
    inv = (500000.0 ** (-np.arange(0, 16, 2, dtype=np.float32) / 16.0)).astype(np.float32)
    return cm, inv


_NC_CACHE = {}


def make_in_maps(inputs, ncores=8):
    cm, inv = _consts()
    f = lambda a: np.ascontiguousarray(np.asarray(a, dtype=np.float32))
    shared = {
        "ln_in_g": f(inputs["ln_in_g"]), "ln_in_b": f(inputs["ln_in_b"]),
        "w_in": f(inputs["w_in"][0]), "b_in": f(inputs["b_in"][0]),
        "sinks": f(inputs["attn_sinks"][0]),
        "sgu_ln_g": f(inputs["sgu_ln_g"][0]), "sgu_ln_b": f(inputs["sgu_ln_b"][0]),
        "sgu_w": f(inputs["sgu_w"][0]), "sgu_b": f(inputs["sgu_b"][0]),
        "w_out": f(inputs["w_out"][0]), "b_out": f(inputs["b_out"][0]),
        "ln_mix_g": f(inputs["ln_mix_g"][0]), "ln_mix_b": f(inputs["ln_mix_b"][0]),
        "w_gate": f(inputs["w_gate"][0]), "w_up": f(inputs["w_up"][0]), "w_down": f(inputs["w_down"][0]),
        "ln_ffn_g": f(inputs["ln_ffn_g"][0]), "ln_ffn_b": f(inputs["ln_ffn_b"][0]),
        "cmask": cm, "invf": inv,
    }
    xs = np.asarray(inputs["x"], dtype=np.float32)
    ps_ = np.asarray(inputs["positions"]).astype(np.int32)
    maps = []
    for c in range(ncores):
        m = dict(shared)
        m["x"] = np.ascontiguousarray(xs[c])
        m["pos"] = np.ascontiguousarray(ps_[c])
        maps.append(m)
    return maps


def kernel(**inputs):
    if "nc" not in _NC_CACHE:
        _NC_CACHE["nc"] = build_nc(8)
    nc = _NC_CACHE["nc"]
    in_maps = make_in_maps(inputs, 8)
    res = run_bass_kernel_spmd(nc, in_maps, core_ids=list(range(8)))
    return np.stack([np.asarray(r["y"], dtype=np.float32) for r in res.results], axis=0)
```
